# Optimizing a Trainium2 kernel written in Bass

```python
import math
import jax, jax.numpy as jnp
from jax import lax
import numpy as np

D_MODEL = 2048
BATCH = 4
SEQ = 8192
DEPTH = 1

N_META = 16
CHUNK = 64
EPS = 1e-6
HG_HEADS = 8
HG_DK = 128
HG_DV = 128
HG_KEY_W = HG_HEADS * HG_DK
HG_VAL_W = HG_HEADS * HG_DV
GD_HEADS = 8
GD_DK = 128
GD_DV = 128
GD_KEY_W = GD_HEADS * GD_DK
GD_VAL_W = GD_HEADS * GD_DV
CONV_K = 4
GD_CONV_DIM = 2 * GD_KEY_W + GD_VAL_W
D_FF = ((-(-8 * D_MODEL // 3) + 255) // 256) * 256
SPLIT_SIZES = (HG_KEY_W, HG_KEY_W, HG_VAL_W, HG_VAL_W,
               GD_KEY_W, GD_KEY_W, GD_VAL_W, GD_VAL_W, GD_HEADS, GD_HEADS,
               D_MODEL, D_MODEL)
SPLIT_POINTS = tuple(int(s) for s in np.cumsum(SPLIT_SIZES)[:-1])
IN_DIM = int(sum(SPLIT_SIZES))

kernel_name = 'hybrid_hgrn2_gdn_block'


def rms_norm(x, w):
    xf = x.astype(jnp.float32)
    y = xf * lax.rsqrt(jnp.mean(xf * xf, axis=-1, keepdims=True) + EPS)
    return (y * w.astype(jnp.float32)).astype(x.dtype)


def to_heads(t, heads, d):
    bsz, length, _ = t.shape
    return t.reshape(bsz, length, heads, d).transpose(0, 2, 1, 3)


def l2_normalize(t):
    return t * lax.rsqrt(jnp.sum(t * t, axis=-1, keepdims=True) + EPS)


def gated_head_norm(o, gate, w):
    bsz, heads, length, d = o.shape
    o = jnp.swapaxes(o, 1, 2)
    o = o * lax.rsqrt(jnp.mean(o * o, axis=-1, keepdims=True) + EPS) * w.astype(jnp.float32)
    g = gate.astype(jnp.float32).reshape(bsz, length, heads, d)
    return (o * jax.nn.silu(g)).reshape(bsz, length, heads * d)


def causal_depthwise_conv(x, w):
    channels = x.shape[-1]
    return lax.conv_general_dilated(
        x, w[:, None, :].astype(x.dtype), window_strides=(1,), padding=[(CONV_K - 1, 0)],
        dimension_numbers=('NWC', 'WIO', 'NWC'), feature_group_count=channels)


def run_chunked(step, state, xs):
    meta = tuple(t[:, :, :N_META] for t in xs)
    real = tuple(t[:, :, N_META:] for t in xs)
    state, o_meta = step(state, meta)
    n_chunks = real[0].shape[2] // CHUNK

    def to_chunks(t):
        t = t.reshape(t.shape[:2] + (n_chunks, CHUNK) + t.shape[3:])
        return jnp.moveaxis(t, 2, 0)

    _, o_real = lax.scan(step, state, tuple(to_chunks(t) for t in real))
    o_real = jnp.moveaxis(o_real, 0, 2)
    o_real = o_real.reshape(o_real.shape[:2] + (n_chunks * CHUNK,) + o_real.shape[4:])
    return jnp.concatenate([o_meta, o_real], axis=2)


def hgrn2_chunk(state, inp):
    q, k, v, log_f = inp
    c = q.shape[2]
    b = jnp.cumsum(log_f, axis=2)
    mask = jnp.tril(jnp.ones((c, c), dtype=bool))
    diff = b[:, :, :, None, :] - b[:, :, None, :, :]
    decay = jnp.where(mask[:, :, None], jnp.exp(jnp.minimum(diff, 0.0)), 0.0)
    scores = jnp.einsum('bhtd,bhjd,bhtjd->bhtj', q, k, decay)
    o = (jnp.einsum('bhtd,bhde->bhte', q * jnp.exp(b), state)
         + jnp.einsum('bhtj,bhje->bhte', scores, v))
    b_last = b[:, :, -1:, :]
    new_state = (jnp.exp(b_last[:, :, 0, :])[..., None] * state
                 + jnp.einsum('bhjd,bhje->bhde', k * jnp.exp(b_last - b), v))
    return new_state, o


def gdn_chunk(state, inp):
    q, k, v, g, beta = inp
    c = q.shape[2]
    gc = jnp.cumsum(g, axis=-1)
    diff = gc[..., :, None] - gc[..., None, :]
    rel = jnp.exp(jnp.minimum(diff, 0.0))
    strict = jnp.tril(jnp.ones((c, c), dtype=bool), -1)
    incl = jnp.tril(jnp.ones((c, c), dtype=bool))
    kk = jnp.einsum('bhid,bhjd->bhij', k, k)
    a = jnp.where(strict, beta[..., :, None] * kk * rel, 0.0)
    eye = jnp.eye(c, dtype=q.dtype)
    rhs = jnp.concatenate([(beta * jnp.exp(gc))[..., None] * k, beta[..., None] * v], axis=-1)
    wu = lax.linalg.triangular_solve(eye + a, rhs, left_side=True, lower=True)
    w, u = wu[..., :GD_DK], wu[..., GD_DK:]
    v_new = u - jnp.einsum('bhtd,bhde->bhte', w, state)
    attn = jnp.where(incl, jnp.einsum('bhtd,bhjd->bhtj', q, k) * rel, 0.0)
    o = (jnp.einsum('bhtd,bhde->bhte', q * jnp.exp(gc)[..., None], state)
         + jnp.einsum('bhtj,bhje->bhte', attn, v_new))
    g_last = gc[..., -1:]
    new_state = (jnp.exp(g_last)[..., None] * state
                 + jnp.einsum('bhjd,bhje->bhde', k * jnp.exp(g_last - gc)[..., None], v_new))
    return new_state, o


def hgrn2_mixer(q_raw, f_raw, i_raw, g_raw, lower_bound, norm_w):
    bsz = q_raw.shape[0]
    f32 = jnp.float32
    fs = f_raw.astype(f32)
    lb = lower_bound.astype(f32)
    q = jax.nn.silu(q_raw.astype(f32))
    log_f = jnp.log(lb + (1.0 - lb) * jax.nn.sigmoid(fs))
    k = (1.0 - lb) * jax.nn.sigmoid(-fs)
    v = i_raw.astype(f32)
    q = to_heads(q, HG_HEADS, HG_DK)
    k = to_heads(k, HG_HEADS, HG_DK)
    log_f = to_heads(log_f, HG_HEADS, HG_DK)
    v = to_heads(v, HG_HEADS, HG_DV)
    state0 = jnp.zeros((bsz, HG_HEADS, HG_DK, HG_DV), f32)
    o = run_chunked(hgrn2_chunk, state0, (q, k, v, log_f))
    return gated_head_norm(o, g_raw, norm_w)


def gated_deltanet_mixer(q_raw, k_raw, v_raw, z_raw, a_raw, b_raw, conv_w, a_log, dt_bias, norm_w):
    bsz = q_raw.shape[0]
    f32 = jnp.float32
    qkv = causal_depthwise_conv(jnp.concatenate([q_raw, k_raw, v_raw], axis=-1), conv_w)
    qkv = jax.nn.silu(qkv.astype(f32))
    q = l2_normalize(to_heads(qkv[..., :GD_KEY_W], GD_HEADS, GD_DK)) * (GD_DK ** -0.5)
    k = l2_normalize(to_heads(qkv[..., GD_KEY_W:2 * GD_KEY_W], GD_HEADS, GD_DK))
    v = to_heads(qkv[..., 2 * GD_KEY_W:], GD_HEADS, GD_DV)
    beta = jnp.swapaxes(jax.nn.sigmoid(b_raw.astype(f32)), 1, 2)
    g = -jnp.exp(a_log.astype(f32)) * jax.nn.softplus(a_raw.astype(f32) + dt_bias.astype(f32))
    g = jnp.swapaxes(g, 1, 2)
    state0 = jnp.zeros((bsz, GD_HEADS, GD_DK, GD_DV), f32)
    o = run_chunked(gdn_chunk, state0, (q, k, v, g, beta))
    return gated_head_norm(o, z_raw, norm_w)


def hybrid_mixer(h, norm_w, w_in, lower_bound, hg_norm_w, conv_w, a_log, dt_bias, gd_norm_w,
                 w_branch_a, w_branch_b, w_out):
    xn = rms_norm(h, norm_w)
    proj = jnp.einsum('bld,de->ble', xn, w_in)
    (hq, hf, hi, hg, gq, gk, gv, gz, ga, gb, gate_a, gate_b) = jnp.split(proj, SPLIT_POINTS, axis=-1)
    o_a = hgrn2_mixer(hq, hf, hi, hg, lower_bound, hg_norm_w).astype(h.dtype)
    o_b = gated_deltanet_mixer(gq, gk, gv, gz, ga, gb, conv_w, a_log, dt_bias, gd_norm_w).astype(h.dtype)
    merged = jax.nn.sigmoid(gate_a) * (o_a @ w_branch_a) + jax.nn.sigmoid(gate_b) * (o_b @ w_branch_b)
    return merged @ w_out


def swiglu_ffn(x, w_in, w_out):
    gate, up = jnp.split(x @ w_in, 2, axis=-1)
    return (jax.nn.silu(gate) * up) @ w_out


def setup_inputs(seed: int = 0) -> dict:
    key = jax.random.key(seed)
    ks = jax.random.split(key, 17)
    f32 = jnp.float32

    def nrm(k, shape, scale):
        return jax.random.normal(k, shape, f32) * scale

    x = nrm(ks[0], (BATCH, SEQ, D_MODEL), 1.0)
    meta_tokens = nrm(ks[1], (N_META, D_MODEL), 1.0)
    lb_logits = nrm(ks[2], (DEPTH + 1, HG_KEY_W), 0.5)
    mix_norm_w = 1.0 + nrm(ks[3], (DEPTH, D_MODEL), 0.02)
    w_in = nrm(ks[4], (DEPTH, D_MODEL, IN_DIM), D_MODEL ** -0.5)
    hg_norm_w = 1.0 + nrm(ks[5], (DEPTH, HG_DV), 0.02)
    gd_conv_w = nrm(ks[6], (DEPTH, CONV_K, GD_CONV_DIM), CONV_K ** -0.5)
    gd_a_log = jnp.log(jax.random.uniform(ks[7], (DEPTH, GD_HEADS), f32, 1.0, 16.0))
    dt = jnp.exp(jax.random.uniform(ks[8], (DEPTH, GD_HEADS), f32, math.log(1e-3), math.log(1e-1)))
    gd_dt_bias = dt + jnp.log(-jnp.expm1(-dt))
    gd_norm_w = 1.0 + nrm(ks[9], (DEPTH, GD_DV), 0.02)
    w_branch_a = nrm(ks[10], (DEPTH, HG_VAL_W, D_MODEL), HG_VAL_W ** -0.5)
    w_branch_b = nrm(ks[11], (DEPTH, GD_VAL_W, D_MODEL), GD_VAL_W ** -0.5)
    w_out = nrm(ks[12], (DEPTH, D_MODEL, D_MODEL), D_MODEL ** -0.5)
    ffn_norm_w = 1.0 + nrm(ks[13], (DEPTH, D_MODEL), 0.02)
    w_ffn_in = nrm(ks[14], (DEPTH, D_MODEL, 2 * D_FF), D_MODEL ** -0.5)
    w_ffn_out = nrm(ks[15], (DEPTH, D_FF, D_MODEL), D_FF ** -0.5)
    final_norm_w = 1.0 + nrm(ks[16], (D_MODEL,), 0.02)
    return {'x': x, 'meta_tokens': meta_tokens, 'lb_logits': lb_logits, 'mix_norm_w': mix_norm_w,
            'w_in': w_in, 'hg_norm_w': hg_norm_w, 'gd_conv_w': gd_conv_w, 'gd_a_log': gd_a_log,
            'gd_dt_bias': gd_dt_bias, 'gd_norm_w': gd_norm_w, 'w_branch_a': w_branch_a,
            'w_branch_b': w_branch_b, 'w_out': w_out, 'ffn_norm_w': ffn_norm_w,
            'w_ffn_in': w_ffn_in, 'w_ffn_out': w_ffn_out, 'final_norm_w': final_norm_w}


def reference(x, meta_tokens, lb_logits, mix_norm_w, w_in, hg_norm_w, gd_conv_w, gd_a_log,
              gd_dt_bias, gd_norm_w, w_branch_a, w_branch_b, w_out, ffn_norm_w, w_ffn_in,
              w_ffn_out, final_norm_w):
    bsz = x.shape[0]
    meta = jnp.broadcast_to(meta_tokens[None].astype(x.dtype), (bsz, N_META, D_MODEL))
    h = jnp.concatenate([meta, x], axis=1)
    lower_bounds = jnp.cumsum(jax.nn.softmax(lb_logits.astype(jnp.float32), axis=0), axis=0)
    for layer in range(DEPTH):
        h = h + hybrid_mixer(h, mix_norm_w[layer], w_in[layer], lower_bounds[layer],
                             hg_norm_w[layer], gd_conv_w[layer], gd_a_log[layer],
                             gd_dt_bias[layer], gd_norm_w[layer], w_branch_a[layer],
                             w_branch_b[layer], w_out[layer])
        if layer == DEPTH - 1:
            h = h[:, N_META:]
        h = h + swiglu_ffn(rms_norm(h, ffn_norm_w[layer]), w_ffn_in[layer], w_ffn_out[layer])
    return rms_norm(h, final_norm_w)
```

```python
import numpy as np
from contextlib import ExitStack
import concourse.bass as bass
import concourse.mybir as mybir
from concourse.bass_utils import run_bass_kernel_spmd

F32 = mybir.dt.float32
BF16 = mybir.dt.bfloat16
AF = mybir.ActivationFunctionType
ALU = mybir.AluOpType

ENGS = ("tensor", "vector", "scalar", "gpsimd", "sync")
NDMA = 12

D = 2048
NH = 8
T = 512
T2 = 256
NT_FULL = 17
DFF = 5632
IN_DIM = 12304
EPS = 1e-6
NEG = -30000.0


class V:
    def __init__(self, ap, res):
        self.ap = ap
        self.res = res

    def __getitem__(self, k):
        return V(self.ap[k], self.res)

    def rearrange(self, *a, **k):
        return V(self.ap.rearrange(*a, **k), self.res)

    def to_broadcast(self, *a, **k):
        return V(self.ap.to_broadcast(*a, **k), self.res)

    def unsqueeze(self, *a, **k):
        return V(self.ap.unsqueeze(*a, **k), self.res)

    @property
    def shape(self):
        return self.ap.shape


def A(x):
    return x.ap if isinstance(x, V) else x


class Sched:
    def __init__(self, nc):
        self.nc = nc
        self.prog = {e: [] for e in ENGS}
        self.cnt = {e: 0 for e in ENGS}
        self.dma_i = {e: 0 for e in ENGS}
        self.waited = {e: {} for e in ENGS}
        self.last_w = {}
        self.reads = {}
        self.kids = {}
        self.stack = ExitStack()
        self.n_inst = 0
        self.n_mm = 0
        self.marks = []

    def sb(self, name, shape, dtype=F32):
        return self.stack.enter_context(self.nc.sbuf_tensor(name, list(shape), dtype))

    def ps(self, name, shape, dtype=F32):
        return self.stack.enter_context(self.nc.psum_tensor(name, list(shape), dtype))

    def _need(self, eng, tok):
        if tok is None:
            return
        key, val = tok
        if key == ("e", "tensor") and eng == "tensor":
            return
        w = self.waited[eng]
        if w.get(key, 0) >= val:
            return
        w[key] = val
        self.prog[eng].append(("wait", key, val))

    def _rel(self, r):
        if "." in r:
            p = r.split(".")[0]
            self.kids.setdefault(p, set()).add(r)
            return (r, p)
        return (r,) + tuple(self.kids.get(r, ()))

    def _deps(self, eng, rd, wr):
        for r0 in rd:
            for r in self._rel(r0):
                self._need(eng, self.last_w.get(r))
        for r0 in wr:
            for r in self._rel(r0):
                self._need(eng, self.last_w.get(r))
                for key, val in self.reads.get(r, {}).items():
                    self._need(eng, (key, val))

    def _commit(self, tok, rd, wr):
        key, val = tok
        for r in rd:
            d = self.reads.setdefault(r, {})
            if d.get(key, 0) < val:
                d[key] = val
        for r in wr:
            self.last_w[r] = tok
            self.reads[r] = {}

    @staticmethod
    def _names(aps):
        out = []
        for a in aps:
            if a is None or isinstance(a, (int, float)):
                continue
            if isinstance(a, str):
                out.append(a)
            elif isinstance(a, V):
                out.append(a.res)
            else:
                out.append(a.name)
        return out

    def op(self, eng, fn, rd, wr, inc=True):
        rd = self._names(rd)
        wr = self._names(wr)
        self._deps(eng, rd, wr)
        self.n_inst += 1
        if inc:
            self.cnt[eng] += 1
            tok = (("e", eng), self.cnt[eng])
            self.prog[eng].append(("inst", fn, tok))
        else:
            tok = (("e", eng), self.cnt[eng] + 1)
            self.prog[eng].append(("inst", fn, None))
        self._commit(tok, rd, wr)
        return tok

    def dma(self, eng, out, in_, **kw):
        rd = self._names([in_])
        wr = self._names([out])
        i = self.dma_i[eng]
        self.dma_i[eng] += 1
        slot, rnd = i % NDMA, i // NDMA
        key = ("d", eng, slot)
        if rnd > 0:
            self._need(eng, (key, 16 * rnd))
        self._deps(eng, rd, wr)
        tok = (key, 16 * (rnd + 1))
        self.n_inst += 1
        self.prog[eng].append(("dma", (A(out), A(in_), kw), tok))
        self._commit(tok, rd, wr)
        return tok

    def mark(self, name):
        self.marks.append((name, self.n_mm))

    def mm(self, out, lhsT, rhs, start=True, stop=True, inc=True):
        self.n_mm += 1
        o, l, r = A(out), A(lhsT), A(rhs)
        return self.op("tensor", lambda e: e.matmul(o, l, r, start=start, stop=stop),
                       [lhsT, rhs], [out], inc=inc)

    def tr(self, out, in_, ident, inc=True):
        self.n_mm += 1
        o, i, d = A(out), A(in_), A(ident)
        return self.op("tensor", lambda e: e.transpose(o, i, d), [in_, ident], [out], inc=inc)

    def act(self, out, in_, func, bias=None, scale=None, accum_out=None):
        kw = {}
        rd = [in_]
        if bias is not None:
            kw["bias"] = A(bias)
            rd.append(bias)
        if scale is not None:
            kw["scale"] = A(scale)
            rd.append(scale)
        wr = [out]
        if accum_out is not None:
            kw["accum_out"] = A(accum_out)
            wr.append(accum_out)
        o, i = A(out), A(in_)
        return self.op("scalar", lambda e: e.activation(o, i, func, **kw), rd, wr)

    def tt(self, out, in0, in1, op, eng="vector"):
        o, a, b = A(out), A(in0), A(in1)
        return self.op(eng, lambda e: e.tensor_tensor(o, a, b, op), [in0, in1], [out])

    def ts(self, out, in0, s1, s2=None, op0=ALU.mult, op1=None, eng="vector"):
        o, a, x1, x2 = A(out), A(in0), A(s1), A(s2)
        if op1 is None:
            return self.op(eng, lambda e: e.tensor_scalar(o, a, x1, None, op0), [in0, s1], [out])
        return self.op(eng, lambda e: e.tensor_scalar(o, a, x1, x2, op0, op1), [in0, s1, s2], [out])

    def stt(self, out, in0, scalar, in1, op0, op1, eng="vector"):
        o, a, s, b = A(out), A(in0), A(scalar), A(in1)
        return self.op(eng, lambda e: e.scalar_tensor_tensor(o, a, s, b, op0, op1), [in0, scalar, in1], [out])

    def scan(self, out, d0, d1, init, op0, op1):
        o, a, b = A(out), A(d0), A(d1)
        return self.op("vector", lambda e: e.tensor_tensor_scan(o, a, b, init, op0, op1), [d0, d1], [out])

    def copy(self, out, in_, eng="vector"):
        o, i = A(out), A(in_)
        if eng == "scalar":
            return self.op(eng, lambda e: e.copy(o, i), [in_], [out])
        return self.op(eng, lambda e: e.tensor_copy(o, i), [in_], [out])

    def memset(self, out, val, eng="vector"):
        o = A(out)
        return self.op(eng, lambda e: e.memset(o, val), [], [out])

    def final_wait(self, eng, toks):
        for t in toks:
            self._need(eng, t)

    def emit(self):
        nc = self.nc
        sems = {}

        def sem(key):
            if key not in sems:
                nm = "s_" + "_".join(str(k) for k in key)
                sems[key] = self.stack.enter_context(nc.semaphore(nm))
            return sems[key]

        for e in ENGS:
            for it in self.prog[e]:
                if it[0] == "wait":
                    sem(it[1])
                elif it[2] is not None:
                    sem(it[2][0])
        with nc.Block() as block:
            def replay(engname):
                def body(e):
                    for it in self.prog[engname]:
                        if it[0] == "wait":
                            e.wait_ge(sem(it[1]), it[2])
                        elif it[0] == "inst":
                            ins = it[1](e)
                            if it[2] is not None:
                                ins.then_inc(sem(it[2][0]), 1)
                        else:
                            out, in_, kw = it[1]
                            e.dma_start(out=out, in_=in_, **kw).then_inc(sem(it[2][0]), 16)
                return body
            block.tensor(replay("tensor"))
            block.vector(replay("vector"))
            block.scalar(replay("scalar"))
            block.gpsimd(replay("gpsimd"))
            block.sync(replay("sync"))


C_ID, C_MI, C_NI, C_NIT, C_MS, C_MST, C_ONE = [i * 128 for i in range(7)]
C_RST = 7 * 128
C_BF_COLS = C_RST + 512


def make_consts():
    c = np.zeros((128, C_BF_COLS), np.float32)
    j = np.arange(128)[:, None]
    t = np.arange(128)[None, :]
    same = (j // 64) == (t // 64)
    incl = same & (j <= t)
    strict = same & (j < t)
    c[:, C_ID:C_ID + 128] = np.eye(128)
    c[:, C_MI:C_MI + 128] = incl
    c[:, C_NI:C_NI + 128] = np.where(incl, 0.0, NEG)
    c[:, C_NIT:C_NIT + 128] = np.where(incl.T, 0.0, NEG)
    c[:, C_MS:C_MS + 128] = -1.0 * strict
    c[:, C_MST:C_MST + 128] = -1.0 * strict.T
    c[:, C_ONE:C_ONE + 128] = 1.0
    r = np.ones(512, np.float32)
    r[::64] = 0.0
    c[:, C_RST:C_RST + 512] = r[None, :]
    import ml_dtypes
    cb = c.astype(ml_dtypes.bfloat16)
    cf = np.zeros((8, 8 + 8 * 128), np.float32)
    cf[:, 0:8] = np.eye(8)
    for h in range(8):
        cf[h, 8 + h * 128: 8 + (h + 1) * 128] = 1.0
    return cb, cf


class _Stop(Exception):
    pass


def build(NT=NT_FULL, debug=False, stop_at=None):
    nc = bass.Bass("TRN2", target_bir_lowering=False)
    S = Sched(nc)
    NTOK = NT * T

    def din(name, shape, dt=F32):
        return nc.dram_tensor(name, list(shape), dt, kind="ExternalInput").ap()

    xs = din("xs", [NTOK, D])
    w_in = din("w_in", [D, IN_DIM])
    w_ba = din("w_ba", [1024, D])
    w_bb = din("w_bb", [1024, D])
    w_o = din("w_o", [D, D])
    w_fi = din("w_fi", [D, 2 * DFF])
    w_fo = din("w_fo", [DFF, D])
    w_ab = din("w_ab", [D, 16])
    lbl = din("lbl", [128, 2, 8])
    nw3 = din("nw3", [128, 2, 16])
    fnw = din("fnw", [1, D])
    hnw = din("hnw", [128, 2])
    cw = din("cw", [128, 4, 24])
    alog = din("alog", [8, 1])
    dtb = din("dtb", [8, 1])
    cbf = din("cbf", [128, C_BF_COLS], BF16)
    cf32 = din("cf32", [8, 8 + 8 * 128])
    out = nc.dram_tensor("out", [max(NT - 1, 1) * T2, D], F32, kind="ExternalOutput").ap()
    if debug:
        dbg_o = nc.dram_tensor("dbg_o", [NT, 128, 16 * T2], BF16, kind="ExternalOutput").ap()

    def dscr(name, shape):
        return nc.dram_tensor(name, list(shape), BF16, kind="Internal").ap()
    wb_in = dscr("wb_in", [D, IN_DIM])
    wb_ba = dscr("wb_ba", [1024, D])
    wb_bb = dscr("wb_bb", [1024, D])
    wb_o = dscr("wb_o", [D, D])
    wb_fi = dscr("wb_fi", [D, 2 * DFF])
    wb_fo = dscr("wb_fo", [DFF, D])

    def convert(dst, src, rows):
        R, C = src.shape
        for r0 in range(0, R, 128):
            r1 = min(R, r0 + 128)
            for c0 in range(0, C, 2048):
                c1 = min(C, c0 + 2048)
                S.dma("gpsimd", dst[r0:r1, c0:c1], src[r0:r1, c0:c1])

    cb = S.sb("cb", [128, C_BF_COLS], BF16)
    cf = S.sb("cf", [8, 8 + 8 * 128], F32)
    ident = cb[:, C_ID:C_ID + 128]
    maskI = cb[:, C_MI:C_MI + 128]
    negI = cb[:, C_NI:C_NI + 128]
    negIT = cb[:, C_NIT:C_NIT + 128]
    mS = cb[:, C_MS:C_MS + 128]
    mST = cb[:, C_MST:C_MST + 128]
    ones_bf = cb[:, C_ONE:C_ONE + 128]
    rst = cb[:, C_RST:C_RST + 512]
    id8 = cf[0:8, 0:8]

    NB = 4
    wring = [S.sb(f"wr{i}", [128, 16, 512], BF16) for i in range(NB)]
    ring_i = [0]

    def load_w(wd, k0, kn, c0, cn, slot=None):
        if slot is None:
            slot = ring_i[0] % NB
            ring_i[0] += 1
        buf = wring[slot]
        src = wd[k0 * 128:(k0 + kn) * 128, c0:c0 + cn].rearrange("(k p) c -> p k c", p=128)
        S.dma("sync", buf[:, 0:kn, 0:cn], src)
        return buf

    whalf = [V(wring[i // 2][:, :, (i % 2) * 256:(i % 2) * 256 + 256], f"wr{i // 2}.{'ab'[i % 2]}") for i in range(8)]

    def load_half(wd, kn, c0, hslot):
        hb_ = whalf[hslot]
        src = wd[0:kn * 128, c0:c0 + 256].rearrange("(k p) c -> p k c", p=128)
        S.dma("sync", hb_[:, 0:kn, :], src)
        return hb_

    h1 = S.sb("h1", [128, 2, D], F32)
    h1v = [V(h1[:, 0, :], "h1.0"), V(h1[:, 1, :], "h1.1")]
    xsb = S.sb("xsb", [128, D], BF16)
    xnT = S.sb("xnT", [128, 16, T], BF16)
    hv = S.sb("hv", [128, 4, 1024], BF16)
    oT = S.sb("oT", [128, 16, T2], BF16)
    Sst = [S.sb(f"S{i}", [128, 128], F32) for i in range(16)]
    Sbf = [S.sb(f"Sb{i}", [128, 128], BF16) for i in range(16)]
    small = S.sb("small", [128, 64], F32)
    lb = S.sb("lb", [128, 8], F32)
    oml = S.sb("oml", [128, 8], F32)
    noml = S.sb("noml", [128, 8], F32)
    lbl_t = S.sb("lbl_t", [128, 2, 8], F32)
    nw3_t = S.sb("nw3_t", [128, 2, 16], F32)
    hnw_t = S.sb("hnw_t", [128, 2], F32)
    cw_t = S.sb("cw_t", [128, 4, 24], F32)
    halo = S.sb("halo", [128, 24, 4], F32)
    fnw_bc = S.sb("fnw_bc", [128, D], F32)
    wab32 = S.sb("wab32", [128, 16, 16], F32)
    wab = S.sb("wab", [128, 16, 16], BF16)
    gsc = S.sb("gsc", [8, 4], F32)
    epsc = S.sb("epsc", [128, 2], F32)
    gT = S.sb("gT", [8, T], F32)
    gcT = S.sb("gcT", [8, T], F32)
    beT = S.sb("beT", [8, T], F32)
    glT = S.sb("glT", [8, T], F32)
    tok = S.sb("tok", [128, 4, 24], F32)
    tok2 = S.sb("tok2", [128, 4, 32], F32)
    ss4 = S.sb("ss4", [128, 8], F32)
    actT = S.sb("actT", [128, 44, T2], BF16)
    scr = actT[:].rearrange("p f t -> p (f t)")
    msc2 = S.sb("msc2", [128, 13312], BF16)
    SCR_NAMES = ["actT"]
    _off = {"scr": 0, "msc2": 0}

    def carve(region, n_bf16, tag, shape=None, f32=False):
        base = scr if region == "scr" else msc2
        o = _off[region]
        _off[region] = o + n_bf16
        assert _off[region] <= (11264 if region == "scr" else 13312), (region, tag, _off[region])
        ap = base[:, o:o + n_bf16]
        if f32:
            ap = ap.bitcast(F32)
        if shape is not None:
            ap = ap.rearrange("p (a b) -> p a b", b=shape[-1])
        if region == "scr" and tag not in SCR_NAMES:
            SCR_NAMES.append(tag)
        return V(ap, tag)

    def carve_reset():
        _off["scr"] = 0
        _off["msc2"] = 0

    H_QT, H_KT, H_KTOK, H_SCT = [], [], [], []
    for hh in range(4):
        H_QT.append(carve("scr", 512, f"h.qt{hh}"))
        H_KT.append(carve("scr", 512, f"h.kt{hh}"))
        H_KTOK.append(carve("scr", 512, f"h.ktok{hh}", shape=[4, 128]))
        H_SCT.append(carve("scr", 256, f"h.sct{hh}", shape=[2, 128]))
    H_TMP = []
    for st in range(2):
        H_TMP.append([carve("msc2", 1024, f"h.t{st}.{i}", f32=True) for i in range(4)]
                     + [carve("msc2", 512, f"h.kh{st}")])
    NSQ = S.sb("nsq", [128, T2], BF16)[:]
    carve_reset()
    G_P4, G_NW4, G_AT4, G_KH4, G_BV4 = [], [], [], [], []
    for hh in range(4):
        G_P4.append(carve("scr", 512, f"h.qt{hh}", shape=[4, 128]))
        G_NW4.append(carve("scr", 512, f"h.kt{hh}", shape=[4, 128]))
        G_KH4.append(carve("scr", 512, f"h.ktok{hh}", shape=[4, 128]))
        G_AT4.append(carve("scr", 256, f"h.sct{hh}", shape=[2, 128]))
    for hh in range(4):
        G_BV4.append(carve("scr", 512, f"g.bv{hh}", shape=[4, 128]))
    G_QG = [carve("scr", 256, f"g.qg{hh}") for hh in range(4)]
    hvf = hv[:].rearrange("p a b -> p (a b)")
    G_QF = [V(hvf[:, i * 1024:(i + 1) * 1024].bitcast(F32), f"hv.qf{i}") for i in range(2)]
    G_KF = [V(hvf[:, 2048 + i * 1024:2048 + (i + 1) * 1024].bitcast(F32), f"hv.kf{i}") for i in range(2)]
    G_VT = [carve("msc2", 512, f"g.vt{i}") for i in range(2)]
    G_VACC = carve("msc2", 1024, "g.vacc", f32=True)
    G_GCB = carve("msc2", 1024, "g.gcb", f32=True)
    G_QNT, G_KNT, G_KBT, G_LNRQ, G_LNRK, G_EGC = [carve("msc2", 512, f"g.s{i}") for i in range(6)]
    G_DM = carve("msc2", 1024, "g.dm", shape=[4, 128], f32=True)
    G_DT = carve("msc2", 1024, "g.dt", shape=[4, 128], f32=True)
    G_X4, G_XT4, G_XA4, G_XB4, G_KB4 = [carve("msc2", 512, f"g.x{i}", shape=[4, 128]) for i in range(5)]
    G_VN = [carve("msc2", 128, f"g.vn{hh}") for hh in range(4)]
    MSC2_H = [f"h.t{st}.{i}" for st in range(2) for i in range(4)] + ["h.kh0", "h.kh1"]
    MSC2_G = (["g.vt0", "g.vt1", "g.vacc", "g.gcb"] + [f"g.s{i}" for i in range(6)]
              + ["g.dm", "g.dt"] + [f"g.x{i}" for i in range(5)] + [f"g.vn{hh}" for hh in range(4)])
    fdummy = S.sb("fdummy", [128, 2], F32)

    def fence(names=None):
        S.op("vector", lambda e: e.memset(fdummy[:, 0:1], 0.0), [],
             (SCR_NAMES + MSC2_H + MSC2_G if names is None else names) + [fdummy[:]])
    elast = S.sb("elast", [128, 16, 8], F32)
    mergedT = hv[:].rearrange("p a b -> p (a b)").rearrange("p (m t) -> p m t", t=T2)
    xn2T = xnT[:, :, T2:T]
    tp = [S.sb(f"tp{i}", [128, T2], F32) for i in range(4)]

    pA = S.ps("pA", [128, 512])
    pB = S.ps("pB", [128, 512])
    pT0 = S.ps("pT0", [128, 1024], BF16)
    pT1 = S.ps("pT1", [128, 1024], BF16)
    pO = S.ps("pO", [128, 512])
    pS = S.ps("pS", [128, 512])
    pM = S.ps("pM", [128, 512])
    pX = S.ps("pX", [128, 512])
    hb = [pO, pS, pM, pX]
    pab_i = [0]

    def next_p():
        pab_i[0] += 1
        return pA if pab_i[0] % 2 else pB

    S.dma("sync", cb[:], cbf)
    S.dma("sync", cf[:], cf32)
    S.dma("sync", lbl_t[:], lbl)
    S.dma("sync", nw3_t[:], nw3)
    S.dma("sync", hnw_t[:], hnw)
    S.dma("sync", cw_t[:], cw)
    S.dma("sync", gsc[:, 0:1], alog)
    S.dma("sync", gsc[:, 1:2], dtb)
    S.dma("sync", fnw_bc[:], fnw.to_broadcast([128, D]))
    S.dma("sync", wab32[:], w_ab.rearrange("(k p) c -> p k c", p=128))
    convert(wb_in, w_in, 128)
    convert(wb_ba, w_ba, 256)
    convert(wb_bb, w_bb, 256)
    convert(wb_o, w_o, 256)
    convert(wb_fi, w_fi, 128)
    convert(wb_fo, w_fo, 256)

    n = S.dma_i["gpsimd"]
    S.final_wait("sync", [(("d", "gpsimd", sl), 16 * ((n - 1 - sl) // NDMA + 1)) for sl in range(min(NDMA, n))])
    S.copy(wab[:], wab32[:])
    S.memset(epsc[:, 0:1], EPS)
    S.memset(epsc[:, 1:2], 1.0)
    S.memset(halo[:], 0.0)
    for i in range(16):
        S.memset(Sst[i][:], 0.0, eng="gpsimd")
        S.memset(Sbf[i][:], 0.0, eng="gpsimd")
    S.tt(lb[:], lbl_t[:, 0, :], lbl_t[:, 1, :], ALU.subtract)
    S.act(lb[:], lb[:], AF.Sigmoid)
    S.ts(oml[:], lb[:], -1.0, 1.0, ALU.mult, ALU.add)
    S.ts(noml[:], oml[:], -1.0)
    S.act(gsc[:, 2:3], gsc[:, 0:1], AF.Exp)
    S.ts(gsc[:, 2:3], gsc[:, 2:3], -1.0)
    S.memset(gsc[:, 3:4], 1.0)
    eps_ap = epsc[:, 0:1]
    one_ap = epsc[:, 1:2]

    def proj_fm(ps_ap, wblk, c0, rhs_of_k, nk=16):
        for k in range(nk):
            S.mm(ps_ap, wblk[:, k, c0:c0 + 128], rhs_of_k(k), start=(k == 0), stop=(k == nk - 1),
                 inc=(k == nk - 1))

    def rstd_from_ss(dst, ss_ap, inv_n):
        S.act(dst, ss_ap, AF.Ln, bias=eps_ap[0:ss_ap.shape[0], :], scale=inv_n)
        S.act(dst, dst, AF.Exp, scale=-0.5)

    def norm_transpose(src_v, s, wcol, dstT, sbase):
        ssc = small[:, sbase + s:sbase + s + 1]
        S.act(xsb[:], src_v, AF.Square, accum_out=ssc)
        rs = small[:, sbase + 4 + s:sbase + 5 + s]
        rstd_from_ss(rs, ssc, 1.0 / D)
        S.ts(xsb[:], src_v, rs)
        for half in range(2):
            pt = pT0 if half == 0 else pT1
            for j in range(8):
                k = half * 8 + j
                S.tr(pt[:, j * 128:(j + 1) * 128], xsb[:, k * 128:(k + 1) * 128], ident, inc=(j == 7))
            S.tt(dstT[:, half * 8:half * 8 + 8, s * 128:(s + 1) * 128],
                 pt[:].rearrange("p (k t) -> p k t", t=128),
                 nw3_t[:, wcol, half * 8:half * 8 + 8].unsqueeze(2).to_broadcast([128, 8, 128]),
                 ALU.mult)

    def head_norm_out(po_v, wg_blk, c0, which, dst):
        S.act(NSQ, po_v, AF.Square)
        pss = next_p()
        S.mm(pss[:, 0:T2], ones_bf, NSQ)
        rs = tp[0][:]
        rstd_from_ss(rs, pss[:, 0:T2], 1.0 / 128)
        pg = next_p()
        proj_fm(pg[:, 0:T2], wg_blk, c0, lambda k: xnT[:, k, 0:T2])
        sg = tp[1][:]
        S.act(sg, pg[:, 0:T2], AF.Silu)
        on = tp[2][:]
        S.tt(on, po_v, rs, ALU.mult)
        S.stt(dst, on, hnw_t[:, which:which + 1], sg, ALU.mult, ALU.mult)

    def ckpt(name):
        S.mark(name)
        if stop_at == name:
            raise _Stop()

    def main_body():
      for tile in range(NT):
        tok0 = tile * T
        for s in range(4):
            xv = h1v[s % 2]
            S.dma("sync", xv, xs[tok0 + s * 128: tok0 + (s + 1) * 128, :])
            norm_transpose(xv, s, 0, xnT, 0)

        ckpt(f"x{tile}")
        do_out = tile >= 1
        fence()
        PW = {}
        PW["i0"] = load_w(wb_in, 0, 16, 2048, 512, slot=0)
        PW["q0"] = load_w(wb_in, 0, 16, 0, 512, slot=1)
        PW["f0"] = load_w(wb_in, 0, 16, 1024, 512, slot=2)
        if do_out:
            PW["g0"] = load_w(wb_in, 0, 16, 3072, 512, slot=3)
        for gi in range(2):
            Wq, Wf, Wi, Wg = PW[f"q{gi}"], PW[f"f{gi}"], PW[f"i{gi}"], PW.get(f"g{gi}")
            for p in range(4):
                pv = next_p()
                for k in range(16):
                    S.mm(pv[:], xnT[:, k, p * 128:(p + 1) * 128], Wi[:, k, :], start=(k == 0), stop=(k == 15),
                         inc=(k == 15))
                S.copy(hv[:, p, gi * 512:(gi + 1) * 512], pv[:], eng="scalar")
            if gi == 0:
                PW["i1"] = load_w(wb_in, 0, 16, 2048 + 512, 512, slot=0)
            else:
                PW["gv0"] = load_w(wb_in, 0, 16, 6144, 512, slot=0)

            def h_proj(hh):
                pq = next_p()
                proj_fm(pq[:], Wq, hh * 128, lambda k: xnT[:, k, :])
                pf = next_p()
                proj_fm(pf[:], Wf, hh * 128, lambda k: xnT[:, k, :])
                return pq, pf

            def h_elem(hh, pq, pf):
                hd = gi * 4 + hh
                t0_, t1_, t2_, t3_, kh = H_TMP[hh % 2]
                qt, kt = H_QT[hh], H_KT[hh]
                S.act(t0_, pf[:], AF.Sigmoid)
                S.act(t3_, pq[:], AF.Silu)
                S.act(t1_, t0_, AF.Ln, bias=lb[:, hd:hd + 1], scale=oml[:, hd:hd + 1])
                S.ts(t0_, t0_, noml[:, hd:hd + 1], oml[:, hd:hd + 1], ALU.mult, ALU.add)
                S.scan(t2_, rst, t1_, 0.0, ALU.mult, ALU.add)
                S.act(t1_, t2_, AF.Exp)
                S.act(t2_, t2_, AF.Exp, scale=-1.0)
                S.tt(qt, t3_, t1_, ALU.mult)
                S.tt(kt, t0_, t2_, ALU.mult)
                ebl = t1_.rearrange("p (c k) -> p c k", k=64)[:, :, 63:64]
                S.copy(elast[:, hd, :], t1_.rearrange("p (c k) -> p c k", k=64)[:, :, 63])
                S.tt(kh.rearrange("p (c k) -> p c k", k=64), kt.rearrange("p (c k) -> p c k", k=64),
                     ebl.to_broadcast([128, 8, 64]), ALU.mult)

            def h_tail(hh):
                kh = H_TMP[hh % 2][4]
                for p in range(4):
                    S.tr(pT0[:, p * 128:(p + 1) * 128], kh[:, p * 128:(p + 1) * 128], ident, inc=(p == 3))
                S.copy(H_KTOK[hh], pT0[:, 0:512].rearrange("p (a b) -> p a b", b=128), eng="scalar")
                if do_out:
                    for p in range(2):
                        pc = slice(p * 128, (p + 1) * 128)
                        S.mm(pM[:, pc], H_KT[hh][:, pc], H_QT[hh][:, pc], inc=(p == 1))
                    S.tt(H_SCT[hh], pM[:, 0:256].rearrange("p (a b) -> p a b", b=128),
                         maskI.unsqueeze(1).to_broadcast([128, 2, 128]), ALU.mult)

            pend = h_proj(0)
            for hh in range(4):
                h_elem(hh, *pend)
                if hh < 3:
                    pend = h_proj(hh + 1)
                h_tail(hh)
            if gi == 0:
                PW["q1"] = load_w(wb_in, 0, 16, 512, 512, slot=1)
                PW["f1"] = load_w(wb_in, 0, 16, 1024 + 512, 512, slot=2)
            else:
                PW["gq0"] = load_w(wb_in, 0, 16, 4096, 512, slot=1)
                PW["gk0"] = load_w(wb_in, 0, 16, 5120, 512, slot=2)
            for c in range(8):
                p, r = c // 2, slice((c % 2) * 64, (c % 2) * 64 + 64)
                for hh in range(4):
                    hd = gi * 4 + hh
                    vv = hv[r, p, hd * 128:(hd + 1) * 128]
                    if do_out and c < 4:
                        oc = hb[hh][:, 256 + c * 64:256 + (c + 1) * 64]
                        S.mm(oc, Sbf[hd][:], H_QT[hh][:, c * 64:(c + 1) * 64], start=True, stop=False, inc=False)
                        S.mm(oc, vv, H_SCT[hh][r, p, (c % 2) * 64:(c % 2) * 64 + 64], start=False, stop=True)
                    S.mm(hb[hh][:, 128:256], H_KTOK[hh][r, p, :], vv)
                    S.stt(Sst[hd][:], Sst[hd][:], elast[:, hd, c:c + 1], hb[hh][:, 128:256], ALU.mult, ALU.add)
                    S.copy(Sbf[hd][:], Sst[hd][:], eng="scalar")
            if do_out:
                for hh in range(4):
                    head_norm_out(hb[hh][:, 256:512], Wg, hh * 128, 0, oT[:, gi * 4 + hh, :])
                if gi == 0:
                    PW["g1"] = load_w(wb_in, 0, 16, 3072 + 512, 512, slot=3)
                else:
                    PW["z0"] = load_w(wb_in, 0, 16, 7168, 512, slot=3)

        ckpt(f"h{tile}")
        fence(SCR_NAMES + MSC2_H + MSC2_G)
        for k in range(16):
            S.mm(pS[0:8, :], wab[:, k, 0:8], xnT[:, k, :], start=(k == 0), stop=(k == 15), inc=(k == 15))
        for k in range(16):
            S.mm(pM[0:8, :], wab[:, k, 8:16], xnT[:, k, :], start=(k == 0), stop=(k == 15), inc=(k == 15))
        S.act(gT[:], pS[0:8, :], AF.Exp, bias=gsc[:, 1:2])
        S.act(gT[:], gT[:], AF.Ln, bias=gsc[:, 3:4])
        S.ts(gT[:], gT[:], gsc[:, 2:3])
        S.scan(gcT[:], rst[0:8, :], gT[:], 0.0, ALU.mult, ALU.add)
        S.act(beT[:], pM[0:8, :], AF.Sigmoid)
        gl = gcT[:].rearrange("p (c k) -> p c k", k=64)[:, :, 63:64]
        S.tt(glT[:].rearrange("p (c k) -> p c k", k=64), gl.to_broadcast([8, 8, 64]),
             gcT[:].rearrange("p (c k) -> p c k", k=64), ALU.subtract)
        for p in range(4):
            pc = slice(p * 128, (p + 1) * 128)
            S.mm(pX[:, 0:8], gcT[:, pc], id8)
            S.mm(pX[:, 8:16], beT[:, pc], id8)
            S.mm(pX[:, 16:24], glT[:, pc], id8)
            S.copy(tok[:, p, :], pX[:, 0:24])
        S.ts(tok2[:, :, 0:8], tok[:, :, 0:8], -1.0)
        S.act(tok2[:, :, 24:32], tok[:, :, 0:8], AF.Exp)
        S.tt(tok2[:, :, 8:16], tok2[:, :, 24:32], tok[:, :, 8:16], ALU.mult)
        S.act(tok2[:, :, 16:24], tok[:, :, 16:24], AF.Exp)

        def bc4(ap128):
            return ap128.unsqueeze(1).to_broadcast([128, 4, 128])

        def tokbc(t, col):
            return t[:, :, col:col + 1].to_broadcast([128, 4, 128])

        def run_streams(*gens):
            gens = [g for g in gens if g is not None]
            while gens:
                for g in list(gens):
                    try:
                        next(g)
                    except StopIteration:
                        gens.remove(g)

        def g_s1(gi, hh, W3):
            hd = gi * 4 + hh
            st = hh % 2
            for qi, Wb in enumerate(W3):
                ci = qi * 8 + hd
                pp = next_p()
                proj_fm(pp[:], Wb, hh * 128, lambda k: xnT[:, k, :])
                yield
                acc = (G_QF[st], G_KF[st], G_VACC)[qi]
                S.ts(acc, pp[:], cw_t[:, 3, ci:ci + 1])
                yield
                for kq in range(3):
                    sh = 3 - kq
                    S.stt(acc[:, sh:T], pp[:, 0:T - sh], cw_t[:, kq, ci:ci + 1], acc[:, sh:T], ALU.mult, ALU.add)
                    S.stt(acc[:, 0:sh], halo[:, ci, 4 - sh:4], cw_t[:, kq, ci:ci + 1], acc[:, 0:sh],
                          ALU.mult, ALU.add)
                    yield
                S.copy(halo[:, ci, 1:4], pp[:, T - 3:T])
                yield
            S.act(G_QF[st], G_QF[st], AF.Silu)
            S.act(G_KF[st], G_KF[st], AF.Silu)
            S.act(G_VT[st], G_VACC, AF.Silu)
            yield

        def g_s23(gi, hh):
            hd = gi * 4 + hh
            si = 8 + hd
            st = hh % 2
            qf, kf, vt = G_QF[st], G_KF[st], G_VT[st]
            selh = cf[0:8, 8 + hd * 128: 8 + (hd + 1) * 128]
            S.mm(pX[:], selh, gcT[:])
            S.act(G_QNT, qf, AF.Square)
            S.act(G_KBT, kf, AF.Square)
            yield
            S.copy(G_GCB, pX[:], eng="scalar")
            S.act(G_EGC, pX[:], AF.Exp)
            S.act(elast[:, si, :], pX[:].rearrange("p (c k) -> p c k", k=64)[:, :, 63], AF.Exp)
            S.mm(pO[:], ones_bf, G_QNT)
            S.mm(pS[:], ones_bf, G_KBT)
            yield
            S.act(G_LNRQ, pO[:], AF.Ln, bias=eps_ap)
            S.act(G_LNRK, pS[:], AF.Ln, bias=eps_ap)
            yield
            S.act(G_LNRQ, G_LNRQ, AF.Exp, scale=-0.5)
            S.act(G_LNRK, G_LNRK, AF.Exp, scale=-0.5)
            S.mm(pX[:], selh, beT[:])
            yield
            S.stt(G_QNT, qf, float(128 ** -0.5), G_LNRQ, ALU.mult, ALU.mult)
            S.tt(G_KNT, kf, G_LNRK, ALU.mult)
            yield
            S.tt(G_KBT, G_KNT, pX[:], ALU.mult)
            S.tt(G_QG[hh], G_QNT[:, 0:T2], G_EGC[:, 0:T2], ALU.mult)
            for p in range(4):
                S.tr(pT0[:, p * 128:(p + 1) * 128], G_KNT[:, p * 128:(p + 1) * 128], ident, inc=(p == 3))
            for p in range(4):
                S.tr(pT1[:, p * 128:(p + 1) * 128], vt[:, p * 128:(p + 1) * 128], ident, inc=(p == 3))
            yield
            pk4 = pT0[:, 0:512].rearrange("p (a b) -> p a b", b=128)
            pv4 = pT1[:, 0:512].rearrange("p (a b) -> p a b", b=128)
            for p in range(4):
                pc = slice(p * 128, (p + 1) * 128)
                S.mm(pM[:, pc], G_KNT[:, pc], G_KBT[:, pc], inc=(p == 3))
            for p in range(4):
                pc = slice(p * 128, (p + 1) * 128)
                S.mm(pX[:, pc], G_KNT[:, pc], G_KNT[:, pc], inc=(p == 3))
            if do_out:
                for p in range(2):
                    pc = slice(p * 128, (p + 1) * 128)
                    S.mm(pS[:, pc], G_KNT[:, pc], G_QNT[:, pc], inc=(p == 1))
            S.tt(G_KB4, pk4, tokbc(tok2, 8 + hd), ALU.mult)
            yield
            gcb4 = G_GCB.rearrange("p (a b) -> p a b", b=128)
            S.tt(G_DM, gcb4, bc4(negI), ALU.add)
            S.tt(G_DT, bc4(negIT), gcb4, ALU.subtract)
            yield
            S.tt(G_DM, G_DM, tokbc(tok2, hd), ALU.add)
            S.tt(G_DT, G_DT, tokbc(tok, hd), ALU.add)
            yield
            S.act(G_DM, G_DM, AF.Exp)
            S.act(G_DT, G_DT, AF.Exp)
            S.tt(G_KH4[hh], pk4, tokbc(tok2, 16 + hd), ALU.mult)
            S.tt(G_BV4[hh], pv4, tokbc(tok, 8 + hd), ALU.mult)
            yield
            if do_out:
                S.tt(G_AT4[hh], pS[:, 0:256].rearrange("p (a b) -> p a b", b=128), G_DM[:, 0:2, :], ALU.mult)
            S.tt(G_DT, G_DT, bc4(mST), ALU.mult)
            yield
            S.tt(G_DM, G_DM, bc4(mS), ALU.mult)
            S.tt(G_DT, G_DT, tokbc(tok, 8 + hd), ALU.mult)
            yield
            S.tt(G_X4, pM[:].rearrange("p (a b) -> p a b", b=128), G_DM, ALU.mult)
            S.tt(G_XT4, pX[:].rearrange("p (a b) -> p a b", b=128), G_DT, ALU.mult)
            yield
            P4 = G_P4[hh]
            S.tt(P4, G_X4, bc4(ident), ALU.add)
            xk, xtk = G_X4, G_XT4
            alt = [(G_XA4, G_XB4), (G_X4, G_XT4)]
            for lvl in range(1, 6):
                nx, nxt = alt[(lvl - 1) % 2]
                for p in range(4):
                    S.mm(pX[:, p * 128:(p + 1) * 128], xk[:, p, :], xtk[:, p, :], inc=(p == 3))
                if lvl < 5:
                    for p in range(4):
                        S.mm(pM[:, p * 128:(p + 1) * 128], xtk[:, p, :], xk[:, p, :], inc=(p == 3))
                yield
                S.copy(nxt, pX[:].rearrange("p (a b) -> p a b", b=128))
                if lvl < 5:
                    S.copy(nx, pM[:].rearrange("p (a b) -> p a b", b=128), eng="scalar")
                yield
                for p in range(4):
                    S.mm(pS[:, p * 128:(p + 1) * 128], nxt[:, p, :], P4[:, p, :], inc=(p == 3))
                yield
                S.tt(P4, P4, pS[:].rearrange("p (a b) -> p a b", b=128), ALU.add)
                yield
                xk, xtk = nx, nxt
            for p in range(4):
                S.mm(pM[:, p * 128:(p + 1) * 128], G_KB4[:, p, :], P4[:, p, :], inc=(p == 3))
            yield
            S.ts(G_NW4[hh], pM[:].rearrange("p (a b) -> p a b", b=128), -1.0)
            yield

        for gi in range(2):
            W3 = (PW[f"gq{gi}"], PW[f"gk{gi}"], PW[f"gv{gi}"])
            run_streams(g_s1(gi, 0, W3))
            for hh in range(4):
                run_streams(g_s23(gi, hh), g_s1(gi, hh + 1, W3) if hh < 3 else None)
                if hh == 2 and gi == 0:
                    PW["gq1"] = load_w(wb_in, 0, 16, 4096 + 512, 512, slot=1)
                    PW["gk1"] = load_w(wb_in, 0, 16, 5120 + 512, 512, slot=2)
                    PW["gv1"] = load_w(wb_in, 0, 16, 6144 + 512, 512, slot=0)
            for c in range(8):
                p, r = c // 2, slice((c % 2) * 64, (c % 2) * 64 + 64)
                for hh in range(4):
                    si = 8 + gi * 4 + hh
                    S.mm(hb[hh][:, 0:128], G_P4[hh][r, p, :], G_BV4[hh][r, p, :], start=True, stop=False, inc=False)
                    S.mm(hb[hh][:, 0:128], G_NW4[hh][:, p, :], Sbf[si][:], start=False, stop=True)
                    vn = G_VN[hh]
                    S.copy(vn[r, :], hb[hh][r, 0:128])
                    if do_out and c < 4:
                        oc = hb[hh][:, 256 + c * 64:256 + (c + 1) * 64]
                        S.mm(oc, Sbf[si][:], G_QG[hh][:, c * 64:(c + 1) * 64], start=True, stop=False, inc=False)
                        S.mm(oc, vn[r, :], G_AT4[hh][r, p, (c % 2) * 64:(c % 2) * 64 + 64], start=False, stop=True)
                    S.mm(hb[hh][:, 128:256], G_KH4[hh][r, p, :], vn[r, :])
                    S.stt(Sst[si][:], Sst[si][:], elast[:, si, c:c + 1], hb[hh][:, 128:256], ALU.mult, ALU.add)
                    S.copy(Sbf[si][:], Sst[si][:], eng="scalar")
            if do_out:
                Wz = PW[f"z{gi}"]
                for hh in range(4):
                    head_norm_out(hb[hh][:, 256:512], Wz, hh * 128, 1, oT[:, 8 + gi * 4 + hh, :])
                if gi == 0:
                    PW["z1"] = load_w(wb_in, 0, 16, 7168 + 512, 512, slot=3)

        ckpt(f"g{tile}")
        if debug and tile >= 1:
            S.dma("sync", dbg_o[tile], oT[:].rearrange("p h t -> p (h t)"))
        if tile == 0:
            continue

        for s in range(2):
            S.dma("sync", h1v[s], xs[tok0 + s * 128: tok0 + (s + 1) * 128, :])
        ring_i[0] = 0
        for mb2 in range(8):
            hs0 = 4 * (mb2 % 2)
            Wga = load_half(wb_in, 16, 8208 + mb2 * 256, hs0 + 0)
            Wgb = load_half(wb_in, 16, 10256 + mb2 * 256, hs0 + 1)
            Wba = load_half(wb_ba, 8, mb2 * 256, hs0 + 2)
            Wbb = load_half(wb_bb, 8, mb2 * 256, hs0 + 3)
            for j in range(2):
                m = mb2 * 2 + j
                pga = next_p()
                proj_fm(pga[:, 0:T2], Wga, j * 128, lambda k: xnT[:, k, 0:T2])
                sga = tp[0][:]
                S.act(sga, pga[:, 0:T2], AF.Sigmoid)
                pgb = next_p()
                proj_fm(pgb[:, 0:T2], Wgb, j * 128, lambda k: xnT[:, k, 0:T2])
                sgb = tp[1][:]
                S.act(sgb, pgb[:, 0:T2], AF.Sigmoid)
                proj_fm(pO[:, 0:T2], Wba, j * 128, lambda k: oT[:, k, :], nk=8)
                proj_fm(pS[:, 0:T2], Wbb, j * 128, lambda k: oT[:, 8 + k, :], nk=8)
                t1, t2 = tp[2][:], tp[3][:]
                S.tt(t1, pO[:, 0:T2], sga, ALU.mult)
                S.tt(t2, pS[:, 0:T2], sgb, ALU.mult)
                S.tt(mergedT[:, m, :], t1, t2, ALU.add, eng="gpsimd")
        ckpt(f"m{tile}")
        for nb in range(4):
            Wo = load_w(wb_o, 0, 16, nb * 512, 512)
            for s in range(2):
                po = next_p()
                for m in range(16):
                    S.mm(po[:], mergedT[:, m, s * 128:(s + 1) * 128], Wo[:, m, :], start=(m == 0),
                         stop=(m == 15), inc=(m == 15))
                hs = h1v[s][:, nb * 512:(nb + 1) * 512]
                S.tt(hs, po[:], hs, ALU.add)
        ckpt(f"o{tile}")
        for s in range(2):
            norm_transpose(h1v[s], s, 1, xn2T, 8)
        fence()
        for fb in range(11):
            Wg_ = load_w(wb_fi, 0, 16, fb * 512, 512)
            Wu_ = load_w(wb_fi, 0, 16, DFF + fb * 512, 512)
            for j in range(4):
                f = fb * 4 + j
                pg = next_p()
                proj_fm(pg[:, 0:T2], Wg_, j * 128, lambda k: xn2T[:, k, :])
                sgt = tp[j % 2][:]
                S.act(sgt, pg[:, 0:T2], AF.Silu)
                pu = next_p()
                proj_fm(pu[:, 0:T2], Wu_, j * 128, lambda k: xn2T[:, k, :])
                S.tt(actT[:, f, :], pu[:, 0:T2], sgt, ALU.mult)
        ckpt(f"i{tile}")
        for nb in range(4):
            pys = [pO, pS]
            for kb4 in range(4):
                Wfo = load_w(wb_fo, kb4 * 11, 11, nb * 512, 512)
                for s in range(2):
                    for kk_ in range(11):
                        f = kb4 * 11 + kk_
                        S.mm(pys[s][:], actT[:, f, s * 128:(s + 1) * 128], Wfo[:, kk_, :],
                             start=(f == 0), stop=(f == 43), inc=(kk_ == 10))
            for s in range(2):
                hs = h1v[s][:, nb * 512:(nb + 1) * 512]
                S.tt(hs, pys[s][:], hs, ALU.add)
        fence()
        ckpt(f"f{tile}")
        for s in range(2):
            ssc = small[:, 32 + s:33 + s]
            S.act(xsb[:], h1v[s], AF.Square, accum_out=ssc)
            rs = small[:, 40 + s:41 + s]
            rstd_from_ss(rs, ssc, 1.0 / D)
            S.stt(h1v[s], h1v[s], rs, fnw_bc[:], ALU.mult, ALU.mult)
            r0 = (tile - 1) * T2 + s * 128
            last = S.dma("sync", out[r0:r0 + 128, :], h1v[s])

    try:
        main_body()
    except _Stop:
        pass
    for q in ("gpsimd", "sync"):
        n = S.dma_i[q]
        S.final_wait(q, [(("d", q, sl), 16 * ((n - 1 - sl) // NDMA + 1)) for sl in range(min(NDMA, n))])
    S.emit()
    return nc, S


def prep_inputs(inputs, NT=NT_FULL, seq=None):
    x = np.asarray(inputs["x"], np.float32)
    B = x.shape[0]
    if seq is None:
        seq = x.shape[1]
    meta = np.asarray(inputs["meta_tokens"], np.float32)
    w_in = np.ascontiguousarray(np.asarray(inputs["w_in"], np.float32)[0])
    cbf, cf32 = make_consts()
    shared = {
        "w_in": w_in,
        "w_ba": np.ascontiguousarray(inputs["w_branch_a"][0], np.float32),
        "w_bb": np.ascontiguousarray(inputs["w_branch_b"][0], np.float32),
        "w_o": np.ascontiguousarray(inputs["w_out"][0], np.float32),
        "w_fi": np.ascontiguousarray(inputs["w_ffn_in"][0], np.float32),
        "w_fo": np.ascontiguousarray(inputs["w_ffn_out"][0], np.float32),
        "w_ab": np.ascontiguousarray(w_in[:, 8192:8208]),
        "lbl": np.ascontiguousarray(np.asarray(inputs["lb_logits"], np.float32).reshape(2, 8, 128).transpose(2, 0, 1)),
        "nw3": np.ascontiguousarray(np.stack([np.asarray(inputs["mix_norm_w"], np.float32)[0].reshape(16, 128).T,
                                              np.asarray(inputs["ffn_norm_w"], np.float32)[0].reshape(16, 128).T], axis=1)),
        "fnw": np.ascontiguousarray(np.asarray(inputs["final_norm_w"], np.float32).reshape(1, D)),
        "hnw": np.ascontiguousarray(np.stack([np.asarray(inputs["hg_norm_w"], np.float32)[0],
                                              np.asarray(inputs["gd_norm_w"], np.float32)[0]], axis=1)),
        "cw": np.ascontiguousarray(np.asarray(inputs["gd_conv_w"], np.float32)[0].reshape(4, 24, 128).transpose(2, 0, 1)),
        "alog": np.ascontiguousarray(np.asarray(inputs["gd_a_log"], np.float32)[0].reshape(8, 1)),
        "dtb": np.ascontiguousarray(np.asarray(inputs["gd_dt_bias"], np.float32)[0].reshape(8, 1)),
        "cbf": cbf, "cf32": cf32,
    }
    maps = []
    NTOK = NT * T
    for c in range(2 * B):
        b, par = c // 2, c % 2
        Z = 496 if par == 0 else 240
        st = np.zeros((NTOK, D), np.float32)
        st[Z:Z + 16] = meta
        n = min(seq, NTOK - Z - 16)
        st[Z + 16:Z + 16 + n] = x[b, :n]
        m = dict(shared)
        m["xs"] = st
        maps.append(m)
    return maps


def assemble(results, B, seq, NT=NT_FULL):
    outp = np.zeros((B, seq, D), np.float32)
    for c in range(2 * B):
        b, par = c // 2, c % 2
        o = np.asarray(results[c]["out"], np.float32)
        for tile in range(1, NT):
            r0 = (tile - 1) * T + par * T2
            if r0 >= seq:
                continue
            outp[b, r0:r0 + T2] = o[(tile - 1) * T2: tile * T2]
    return outp


_CACHE = {}


def kernel(**inputs):
    x = np.asarray(inputs["x"])
    B, seq = x.shape[0], x.shape[1]
    if "nc" not in _CACHE:
        _CACHE["nc"] = build(NT_FULL)[0]
    nc = _CACHE["nc"]
    maps = prep_inputs(inputs, NT_FULL)
    res = run_bass_kernel_spmd(nc, maps, core_ids=list(range(2 * B)))
    return assemble(res.results, B, seq, NT_FULL)
```

```python
import numpy as np
from contextlib import ExitStack
import concourse.bass as bass
import concourse.mybir as mybir
from concourse.bass_utils import run_bass_kernel_spmd

F32 = mybir.dt.float32
BF16 = mybir.dt.bfloat16
AF = mybir.ActivationFunctionType
ALU = mybir.AluOpType

ENGS = ("tensor", "vector", "scalar", "gpsimd", "sync")
NDMA = 12

D = 2048
NH = 8
T = 512
T2 = 256
NT_FULL = 17
DFF = 5632
IN_DIM = 12304
EPS = 1e-6
NEG = -30000.0


class V:
    def __init__(self, ap, res):
        self.ap = ap
        self.res = res

    def __getitem__(self, k):
        return V(self.ap[k], self.res)

    def rearrange(self, *a, **k):
        return V(self.ap.rearrange(*a, **k), self.res)

    def to_broadcast(self, *a, **k):
        return V(self.ap.to_broadcast(*a, **k), self.res)

    def unsqueeze(self, *a, **k):
        return V(self.ap.unsqueeze(*a, **k), self.res)

    @property
    def shape(self):
        return self.ap.shape


def A(x):
    return x.ap if isinstance(x, V) else x


class Sched:
    def __init__(self, nc):
        self.nc = nc
        self.prog = {e: [] for e in ENGS}
        self.cnt = {e: 0 for e in ENGS}
        self.dma_i = {e: 0 for e in ENGS}
        self.waited = {e: {} for e in ENGS}
        self.last_w = {}
        self.reads = {}
        self.kids = {}
        self.stack = ExitStack()
        self.n_inst = 0
        self.n_mm = 0
        self.marks = []

    def sb(self, name, shape, dtype=F32):
        return self.stack.enter_context(self.nc.sbuf_tensor(name, list(shape), dtype))

    def ps(self, name, shape, dtype=F32):
        return self.stack.enter_context(self.nc.psum_tensor(name, list(shape), dtype))

    def _need(self, eng, tok):
        if tok is None:
            return
        key, val = tok
        if key == ("e", "tensor") and eng == "tensor":
            return
        w = self.waited[eng]
        if w.get(key, 0) >= val:
            return
        w[key] = val
        self.prog[eng].append(("wait", key, val))

    def _rel(self, r):
        if "." in r:
            p = r.split(".")[0]
            self.kids.setdefault(p, set()).add(r)
            return (r, p)
        return (r,) + tuple(self.kids.get(r, ()))

    def _deps(self, eng, rd, wr):
        for r0 in rd:
            for r in self._rel(r0):
                self._need(eng, self.last_w.get(r))
        for r0 in wr:
            for r in self._rel(r0):
                self._need(eng, self.last_w.get(r))
                for key, val in self.reads.get(r, {}).items():
                    self._need(eng, (key, val))

    def _commit(self, tok, rd, wr):
        key, val = tok
        for r in rd:
            d = self.reads.setdefault(r, {})
            if d.get(key, 0) < val:
                d[key] = val
        for r in wr:
            self.last_w[r] = tok
            self.reads[r] = {}

    @staticmethod
    def _names(aps):
        out = []
        for a in aps:
            if a is None or isinstance(a, (int, float)):
                continue
            if isinstance(a, str):
                out.append(a)
            elif isinstance(a, V):
                out.append(a.res)
            else:
                out.append(a.name)
        return out

    def op(self, eng, fn, rd, wr, inc=True):
        rd = self._names(rd)
        wr = self._names(wr)
        self._deps(eng, rd, wr)
        self.n_inst += 1
        if inc:
            self.cnt[eng] += 1
            tok = (("e", eng), self.cnt[eng])
            self.prog[eng].append(("inst", fn, tok))
        else:
            tok = (("e", eng), self.cnt[eng] + 1)
            self.prog[eng].append(("inst", fn, None))
        self._commit(tok, rd, wr)
        return tok

    def dma(self, eng, out, in_, **kw):
        rd = self._names([in_])
        wr = self._names([out])
        i = self.dma_i[eng]
        self.dma_i[eng] += 1
        slot, rnd = i % NDMA, i // NDMA
        key = ("d", eng, slot)
        if rnd > 0:
            self._need(eng, (key, 16 * rnd))
        self._deps(eng, rd, wr)
        tok = (key, 16 * (rnd + 1))
        self.n_inst += 1
        self.prog[eng].append(("dma", (A(out), A(in_), kw), tok))
        self._commit(tok, rd, wr)
        return tok

    def mark(self, name):
        self.marks.append((name, self.n_mm))

    def mm(self, out, lhsT, rhs, start=True, stop=True, inc=True):
        self.n_mm += 1
        o, l, r = A(out), A(lhsT), A(rhs)
        return self.op("tensor", lambda e: e.matmul(o, l, r, start=start, stop=stop),
                       [lhsT, rhs], [out], inc=inc)

    def tr(self, out, in_, ident, inc=True):
        self.n_mm += 1
        o, i, d = A(out), A(in_), A(ident)
        return self.op("tensor", lambda e: e.transpose(o, i, d), [in_, ident], [out], inc=inc)

    def act(self, out, in_, func, bias=None, scale=None, accum_out=None):
        kw = {}
        rd = [in_]
        if bias is not None:
            kw["bias"] = A(bias)
            rd.append(bias)
        if scale is not None:
            kw["scale"] = A(scale)
            rd.append(scale)
        wr = [out]
        if accum_out is not None:
            kw["accum_out"] = A(accum_out)
            wr.append(accum_out)
        o, i = A(out), A(in_)
        return self.op("scalar", lambda e: e.activation(o, i, func, **kw), rd, wr)

    def tt(self, out, in0, in1, op, eng="vector"):
        o, a, b = A(out), A(in0), A(in1)
        return self.op(eng, lambda e: e.tensor_tensor(o, a, b, op), [in0, in1], [out])

    def ts(self, out, in0, s1, s2=None, op0=ALU.mult, op1=None, eng="vector"):
        o, a, x1, x2 = A(out), A(in0), A(s1), A(s2)
        if op1 is None:
            return self.op(eng, lambda e: e.tensor_scalar(o, a, x1, None, op0), [in0, s1], [out])
        return self.op(eng, lambda e: e.tensor_scalar(o, a, x1, x2, op0, op1), [in0, s1, s2], [out])

    def stt(self, out, in0, scalar, in1, op0, op1, eng="vector"):
        o, a, s, b = A(out), A(in0), A(scalar), A(in1)
        return self.op(eng, lambda e: e.scalar_tensor_tensor(o, a, s, b, op0, op1), [in0, scalar, in1], [out])

    def scan(self, out, d0, d1, init, op0, op1):
        o, a, b = A(out), A(d0), A(d1)
        return self.op("vector", lambda e: e.tensor_tensor_scan(o, a, b, init, op0, op1), [d0, d1], [out])

    def copy(self, out, in_, eng="vector"):
        o, i = A(out), A(in_)
        if eng == "scalar":
            return self.op(eng, lambda e: e.copy(o, i), [in_], [out])
        return self.op(eng, lambda e: e.tensor_copy(o, i), [in_], [out])

    def memset(self, out, val, eng="vector"):
        o = A(out)
        return self.op(eng, lambda e: e.memset(o, val), [], [out])

    def final_wait(self, eng, toks):
        for t in toks:
            self._need(eng, t)

    def emit(self):
        nc = self.nc
        sems = {}

        def sem(key):
            if key not in sems:
                nm = "s_" + "_".join(str(k) for k in key)
                sems[key] = self.stack.enter_context(nc.semaphore(nm))
            return sems[key]

        for e in ENGS:
            for it in self.prog[e]:
                if it[0] == "wait":
                    sem(it[1])
                elif it[2] is not None:
                    sem(it[2][0])
        with nc.Block() as block:
            def replay(engname):
                def body(e):
                    for it in self.prog[engname]:
                        if it[0] == "wait":
                            e.wait_ge(sem(it[1]), it[2])
                        elif it[0] == "inst":
                            ins = it[1](e)
                            if it[2] is not None:
                                ins.then_inc(sem(it[2][0]), 1)
                        else:
                            out, in_, kw = it[1]
                            e.dma_start(out=out, in_=in_, **kw).then_inc(sem(it[2][0]), 16)
                return body
            block.tensor(replay("tensor"))
            block.vector(replay("vector"))
            block.scalar(replay("scalar"))
            block.gpsimd(replay("gpsimd"))
            block.sync(replay("sync"))


C_ID, C_MI, C_NI, C_NIT, C_MS, C_MST, C_ONE = [i * 128 for i in range(7)]
C_RST = 7 * 128
C_BF_COLS = C_RST + 512


def make_consts():
    c = np.zeros((128, C_BF_COLS), np.float32)
    j = np.arange(128)[:, None]
    t = np.arange(128)[None, :]
    same = (j // 64) == (t // 64)
    incl = same & (j <= t)
    strict = same & (j < t)
    c[:, C_ID:C_ID + 128] = np.eye(128)
    c[:, C_MI:C_MI + 128] = incl
    c[:, C_NI:C_NI + 128] = np.where(incl, 0.0, NEG)
    c[:, C_NIT:C_NIT + 128] = np.where(incl.T, 0.0, NEG)
    c[:, C_MS:C_MS + 128] = -1.0 * strict
    c[:, C_MST:C_MST + 128] = -1.0 * strict.T
    c[:, C_ONE:C_ONE + 128] = 1.0
    r = np.ones(512, np.float32)
    r[::64] = 0.0
    c[:, C_RST:C_RST + 512] = r[None, :]
    import ml_dtypes
    cb = c.astype(ml_dtypes.bfloat16)
    cf = np.zeros((8, 8 + 8 * 128), np.float32)
    cf[:, 0:8] = np.eye(8)
    for h in range(8):
        cf[h, 8 + h * 128: 8 + (h + 1) * 128] = 1.0
    return cb, cf


class _Stop(Exception):
    pass


def build(NT=NT_FULL, debug=False, stop_at=None):
    nc = bass.Bass("TRN2", target_bir_lowering=False)
    S = Sched(nc)
    NTOK = NT * T

    def din(name, shape, dt=F32):
        return nc.dram_tensor(name, list(shape), dt, kind="ExternalInput").ap()

    xs = din("xs", [NTOK, D])
    w_in = din("w_in", [D, IN_DIM])
    w_ba = din("w_ba", [1024, D])
    w_bb = din("w_bb", [1024, D])
    w_o = din("w_o", [D, D])
    w_fi = din("w_fi", [D, 2 * DFF])
    w_fo = din("w_fo", [DFF, D])
    w_ab = din("w_ab", [D, 16])
    lbl = din("lbl", [128, 2, 8])
    nw3 = din("nw3", [128, 2, 16])
    fnw = din("fnw", [1, D])
    hnw = din("hnw", [128, 2])
    cw = din("cw", [128, 4, 24])
    alog = din("alog", [8, 1])
    dtb = din("dtb", [8, 1])
    cbf = din("cbf", [128, C_BF_COLS], BF16)
    cf32 = din("cf32", [8, 8 + 8 * 128])
    out = nc.dram_tensor("out", [max(NT - 1, 1) * T2, D], F32, kind="ExternalOutput").ap()
    if debug:
        dbg_o = nc.dram_tensor("dbg_o", [NT, 128, 16 * T2], BF16, kind="ExternalOutput").ap()

    def dscr(name, shape):
        return nc.dram_tensor(name, list(shape), BF16, kind="Internal").ap()
    wb_in = dscr("wb_in", [D, IN_DIM])
    wb_ba = dscr("wb_ba", [1024, D])
    wb_bb = dscr("wb_bb", [1024, D])
    wb_o = dscr("wb_o", [D, D])
    wb_fi = dscr("wb_fi", [D, 2 * DFF])
    wb_fo = dscr("wb_fo", [DFF, D])

    def convert(dst, src, rows):
        R, C = src.shape
        for r0 in range(0, R, 128):
            r1 = min(R, r0 + 128)
            for c0 in range(0, C, 2048):
                c1 = min(C, c0 + 2048)
                S.dma("gpsimd", dst[r0:r1, c0:c1], src[r0:r1, c0:c1])

    cb = S.sb("cb", [128, C_BF_COLS], BF16)
    cf = S.sb("cf", [8, 8 + 8 * 128], F32)
    ident = cb[:, C_ID:C_ID + 128]
    maskI = cb[:, C_MI:C_MI + 128]
    negI = cb[:, C_NI:C_NI + 128]
    negIT = cb[:, C_NIT:C_NIT + 128]
    mS = cb[:, C_MS:C_MS + 128]
    mST = cb[:, C_MST:C_MST + 128]
    ones_bf = cb[:, C_ONE:C_ONE + 128]
    rst = cb[:, C_RST:C_RST + 512]
    id8 = cf[0:8, 0:8]

    NB = 4
    wring = [S.sb(f"wr{i}", [128, 16, 512], BF16) for i in range(NB)]
    ring_i = [0]

    def load_w(wd, k0, kn, c0, cn, slot=None):
        if slot is None:
            slot = ring_i[0] % NB
            ring_i[0] += 1
        buf = wring[slot]
        src = wd[k0 * 128:(k0 + kn) * 128, c0:c0 + cn].rearrange("(k p) c -> p k c", p=128)
        S.dma("sync", buf[:, 0:kn, 0:cn], src)
        return buf

    whalf = [V(wring[i // 2][:, :, (i % 2) * 256:(i % 2) * 256 + 256], f"wr{i // 2}.{'ab'[i % 2]}") for i in range(8)]

    def load_half(wd, kn, c0, hslot):
        hb_ = whalf[hslot]
        src = wd[0:kn * 128, c0:c0 + 256].rearrange("(k p) c -> p k c", p=128)
        S.dma("sync", hb_[:, 0:kn, :], src)
        return hb_

    h1 = S.sb("h1", [128, 2, D], F32)
    h1v = [V(h1[:, 0, :], "h1.0"), V(h1[:, 1, :], "h1.1")]
    xsb = S.sb("xsb", [128, D], BF16)
    xnT = S.sb("xnT", [128, 16, T], BF16)
    hv = S.sb("hv", [128, 4, 1024], BF16)
    oT = S.sb("oT", [128, 16, T2], BF16)
    Sst = [S.sb(f"S{i}", [128, 128], F32) for i in range(16)]
    Sbf = [S.sb(f"Sb{i}", [128, 128], BF16) for i in range(16)]
    small = S.sb("small", [128, 64], F32)
    lb = S.sb("lb", [128, 8], F32)
    oml = S.sb("oml", [128, 8], F32)
    noml = S.sb("noml", [128, 8], F32)
    lbl_t = S.sb("lbl_t", [128, 2, 8], F32)
    nw3_t = S.sb("nw3_t", [128, 2, 16], F32)
    hnw_t = S.sb("hnw_t", [128, 2], F32)
    cw_t = S.sb("cw_t", [128, 4, 24], F32)
    halo = S.sb("halo", [128, 24, 4], F32)
    fnw_bc = S.sb("fnw_bc", [128, D], F32)
    wab32 = S.sb("wab32", [128, 16, 16], F32)
    wab = S.sb("wab", [128, 16, 16], BF16)
    gsc = S.sb("gsc", [8, 4], F32)
    epsc = S.sb("epsc", [128, 2], F32)
    gT = S.sb("gT", [8, T], F32)
    gcT = S.sb("gcT", [8, T], F32)
    beT = S.sb("beT", [8, T], F32)
    glT = S.sb("glT", [8, T], F32)
    tok = S.sb("tok", [128, 4, 24], F32)
    tok2 = S.sb("tok2", [128, 4, 32], F32)
    ss4 = S.sb("ss4", [128, 8], F32)
    actT = S.sb("actT", [128, 44, T2], BF16)
    scr = actT[:].rearrange("p f t -> p (f t)")
    msc2 = S.sb("msc2", [128, 13312], BF16)
    SCR_NAMES = ["actT"]
    _off = {"scr": 0, "msc2": 0}

    def carve(region, n_bf16, tag, shape=None, f32=False):
        base = scr if region == "scr" else msc2
        o = _off[region]
        _off[region] = o + n_bf16
        assert _off[region] <= (11264 if region == "scr" else 13312), (region, tag, _off[region])
        ap = base[:, o:o + n_bf16]
        if f32:
            ap = ap.bitcast(F32)
        if shape is not None:
            ap = ap.rearrange("p (a b) -> p a b", b=shape[-1])
        if region == "scr" and tag not in SCR_NAMES:
            SCR_NAMES.append(tag)
        return V(ap, tag)

    def carve_reset():
        _off["scr"] = 0
        _off["msc2"] = 0

    H_QT, H_KT, H_KTOK, H_SCT = [], [], [], []
    for hh in range(4):
        H_QT.append(carve("scr", 512, f"h.qt{hh}"))
        H_KT.append(carve("scr", 512, f"h.kt{hh}"))
        H_KTOK.append(carve("scr", 512, f"h.ktok{hh}", shape=[4, 128]))
        H_SCT.append(carve("scr", 256, f"h.sct{hh}", shape=[2, 128]))
    H_TMP = []
    for st in range(2):
        H_TMP.append([carve("msc2", 1024, f"h.t{st}.{i}", f32=True) for i in range(4)]
                     + [carve("msc2", 512, f"h.kh{st}")])
    NSQ = S.sb("nsq", [128, T2], BF16)[:]
    carve_reset()
    G_P4, G_NW4, G_AT4, G_KH4, G_BV4 = [], [], [], [], []
    for hh in range(4):
        G_P4.append(carve("scr", 512, f"h.qt{hh}", shape=[4, 128]))
        G_NW4.append(carve("scr", 512, f"h.kt{hh}", shape=[4, 128]))
        G_KH4.append(carve("scr", 512, f"h.ktok{hh}", shape=[4, 128]))
        G_AT4.append(carve("scr", 256, f"h.sct{hh}", shape=[2, 128]))
    for hh in range(4):
        G_BV4.append(carve("scr", 512, f"g.bv{hh}", shape=[4, 128]))
    G_QG = [carve("scr", 256, f"g.qg{hh}") for hh in range(4)]
    hvf = hv[:].rearrange("p a b -> p (a b)")
    G_QF = [V(hvf[:, i * 1024:(i + 1) * 1024].bitcast(F32), f"hv.qf{i}") for i in range(2)]
    G_KF = [V(hvf[:, 2048 + i * 1024:2048 + (i + 1) * 1024].bitcast(F32), f"hv.kf{i}") for i in range(2)]
    G_VT = [carve("msc2", 512, f"g.vt{i}") for i in range(2)]
    G_VACC = carve("msc2", 1024, "g.vacc", f32=True)
    G_GCB = carve("msc2", 1024, "g.gcb", f32=True)
    G_QNT, G_KNT, G_KBT, G_LNRQ, G_LNRK, G_EGC = [carve("msc2", 512, f"g.s{i}") for i in range(6)]
    G_DM = carve("msc2", 1024, "g.dm", shape=[4, 128], f32=True)
    G_DT = carve("msc2", 1024, "g.dt", shape=[4, 128], f32=True)
    G_X4, G_XT4, G_XA4, G_XB4, G_KB4 = [carve("msc2", 512, f"g.x{i}", shape=[4, 128]) for i in range(5)]
    G_VN = [carve("msc2", 128, f"g.vn{hh}") for hh in range(4)]
    MSC2_H = [f"h.t{st}.{i}" for st in range(2) for i in range(4)] + ["h.kh0", "h.kh1"]
    MSC2_G = (["g.vt0", "g.vt1", "g.vacc", "g.gcb"] + [f"g.s{i}" for i in range(6)]
              + ["g.dm", "g.dt"] + [f"g.x{i}" for i in range(5)] + [f"g.vn{hh}" for hh in range(4)])
    fdummy = S.sb("fdummy", [128, 2], F32)

    def fence(names=None):
        S.op("vector", lambda e: e.memset(fdummy[:, 0:1], 0.0), [],
             (SCR_NAMES + MSC2_H + MSC2_G if names is None else names) + [fdummy[:]])
    elast = S.sb("elast", [128, 16, 8], F32)
    mergedT = hv[:].rearrange("p a b -> p (a b)").rearrange("p (m t) -> p m t", t=T2)
    xn2T = xnT[:, :, T2:T]
    tp = [S.sb(f"tp{i}", [128, T2], F32) for i in range(4)]

    pA = S.ps("pA", [128, 512])
    pB = S.ps("pB", [128, 512])
    pT0 = S.ps("pT0", [128, 1024], BF16)
    pT1 = S.ps("pT1", [128, 1024], BF16)
    pO = S.ps("pO", [128, 512])
    pS = S.ps("pS", [128, 512])
    pM = S.ps("pM", [128, 512])
    pX = S.ps("pX", [128, 512])
    hb = [pO, pS, pM, pX]
    pab_i = [0]

    def next_p():
        pab_i[0] += 1
        return pA if pab_i[0] % 2 else pB

    S.dma("sync", cb[:], cbf)
    S.dma("sync", cf[:], cf32)
    S.dma("sync", lbl_t[:], lbl)
    S.dma("sync", nw3_t[:], nw3)
    S.dma("sync", hnw_t[:], hnw)
    S.dma("sync", cw_t[:], cw)
    S.dma("sync", gsc[:, 0:1], alog)
    S.dma("sync", gsc[:, 1:2], dtb)
    S.dma("sync", fnw_bc[:], fnw.to_broadcast([128, D]))
    S.dma("sync", wab32[:], w_ab.rearrange("(k p) c -> p k c", p=128))
    convert(wb_in, w_in, 128)
    convert(wb_ba, w_ba, 256)
    convert(wb_bb, w_bb, 256)
    convert(wb_o, w_o, 256)
    convert(wb_fi, w_fi, 128)
    convert(wb_fo, w_fo, 256)

    n = S.dma_i["gpsimd"]
    S.final_wait("sync", [(("d", "gpsimd", sl), 16 * ((n - 1 - sl) // NDMA + 1)) for sl in range(min(NDMA, n))])
    S.copy(wab[:], wab32[:])
    S.memset(epsc[:, 0:1], EPS)
    S.memset(epsc[:, 1:2], 1.0)
    S.memset(halo[:], 0.0)
    for i in range(16):
        S.memset(Sst[i][:], 0.0, eng="gpsimd")
        S.memset(Sbf[i][:], 0.0, eng="gpsimd")
    S.tt(lb[:], lbl_t[:, 0, :], lbl_t[:, 1, :], ALU.subtract)
    S.act(lb[:], lb[:], AF.Sigmoid)
    S.ts(oml[:], lb[:], -1.0, 1.0, ALU.mult, ALU.add)
    S.ts(noml[:], oml[:], -1.0)
    S.act(gsc[:, 2:3], gsc[:, 0:1], AF.Exp)
    S.ts(gsc[:, 2:3], gsc[:, 2:3], -1.0)
    S.memset(gsc[:, 3:4], 1.0)
    eps_ap = epsc[:, 0:1]
    one_ap = epsc[:, 1:2]

    def proj_fm(ps_ap, wblk, c0, rhs_of_k, nk=16):
        for k in range(nk):
            S.mm(ps_ap, wblk[:, k, c0:c0 + 128], rhs_of_k(k), start=(k == 0), stop=(k == nk - 1),
                 inc=(k == nk - 1))

    def rstd_from_ss(dst, ss_ap, inv_n):
        S.act(dst, ss_ap, AF.Ln, bias=eps_ap[0:ss_ap.shape[0], :], scale=inv_n)
        S.act(dst, dst, AF.Exp, scale=-0.5)

    def norm_transpose(src_v, s, wcol, dstT, sbase):
        ssc = small[:, sbase + s:sbase + s + 1]
        S.act(xsb[:], src_v, AF.Square, accum_out=ssc)
        rs = small[:, sbase + 4 + s:sbase + 5 + s]
        rstd_from_ss(rs, ssc, 1.0 / D)
        S.ts(xsb[:], src_v, rs)
        for half in range(2):
            pt = pT0 if half == 0 else pT1
            for j in range(8):
                k = half * 8 + j
                S.tr(pt[:, j * 128:(j + 1) * 128], xsb[:, k * 128:(k + 1) * 128], ident, inc=(j == 7))
            S.tt(dstT[:, half * 8:half * 8 + 8, s * 128:(s + 1) * 128],
                 pt[:].rearrange("p (k t) -> p k t", t=128),
                 nw3_t[:, wcol, half * 8:half * 8 + 8].unsqueeze(2).to_broadcast([128, 8, 128]),
                 ALU.mult)

    def head_norm_out(po_v, wg_blk, c0, which, dst):
        S.act(NSQ, po_v, AF.Square)
        pss = next_p()
        S.mm(pss[:, 0:T2], ones_bf, NSQ)
        rs = tp[0][:]
        rstd_from_ss(rs, pss[:, 0:T2], 1.0 / 128)
        pg = next_p()
        proj_fm(pg[:, 0:T2], wg_blk, c0, lambda k: xnT[:, k, 0:T2])
        sg = tp[1][:]
        S.act(sg, pg[:, 0:T2], AF.Silu)
        on = tp[2][:]
        S.tt(on, po_v, rs, ALU.mult)
        S.stt(dst, on, hnw_t[:, which:which + 1], sg, ALU.mult, ALU.mult)

    def ckpt(name):
        S.mark(name)
        if stop_at == name:
            raise _Stop()

    def main_body():
      for tile in range(NT):
        tok0 = tile * T
        for s in range(4):
            xv = h1v[s % 2]
            S.dma("sync", xv, xs[tok0 + s * 128: tok0 + (s + 1) * 128, :])
            norm_transpose(xv, s, 0, xnT, 0)

        ckpt(f"x{tile}")
        do_out = tile >= 1
        fence()
        PW = {}
        PW["i0"] = load_w(wb_in, 0, 16, 2048, 512, slot=0)
        PW["q0"] = load_w(wb_in, 0, 16, 0, 512, slot=1)
        PW["f0"] = load_w(wb_in, 0, 16, 1024, 512, slot=2)
        if do_out:
            PW["g0"] = load_w(wb_in, 0, 16, 3072, 512, slot=3)
        for gi in range(2):
            Wq, Wf, Wi, Wg = PW[f"q{gi}"], PW[f"f{gi}"], PW[f"i{gi}"], PW.get(f"g{gi}")
            for p in range(4):
                pv = next_p()
                for k in range(16):
                    S.mm(pv[:], xnT[:, k, p * 128:(p + 1) * 128], Wi[:, k, :], start=(k == 0), stop=(k == 15),
                         inc=(k == 15))
                S.copy(hv[:, p, gi * 512:(gi + 1) * 512], pv[:], eng="scalar")
            if gi == 0:
                PW["i1"] = load_w(wb_in, 0, 16, 2048 + 512, 512, slot=0)
            else:
                PW["gv0"] = load_w(wb_in, 0, 16, 6144, 512, slot=0)

            def h_proj(hh):
                pq = None
                if do_out:
                    pq = next_p()
                    proj_fm(pq[:, 0:T2], Wq, hh * 128, lambda k: xnT[:, k, 0:T2])
                pf = next_p()
                proj_fm(pf[:], Wf, hh * 128, lambda k: xnT[:, k, :])
                return pq, pf

            def h_elem(hh, pq, pf):
                hd = gi * 4 + hh
                t0_, t1_, t2_, t3_, kh = H_TMP[hh % 2]
                qt, kt = H_QT[hh], H_KT[hh]
                S.act(t0_, pf[:], AF.Sigmoid)
                if do_out:
                    S.act(t3_[:, 0:T2], pq[:, 0:T2], AF.Silu)
                S.act(t1_, t0_, AF.Ln, bias=lb[:, hd:hd + 1], scale=oml[:, hd:hd + 1])
                S.ts(t0_, t0_, noml[:, hd:hd + 1], oml[:, hd:hd + 1], ALU.mult, ALU.add)
                S.scan(t2_, rst, t1_, 0.0, ALU.mult, ALU.add)
                S.act(t1_, t2_, AF.Exp)
                S.act(t2_, t2_, AF.Exp, scale=-1.0)
                if do_out:
                    S.tt(qt[:, 0:T2], t3_[:, 0:T2], t1_[:, 0:T2], ALU.mult)
                S.tt(kt, t0_, t2_, ALU.mult)
                ebl = t1_.rearrange("p (c k) -> p c k", k=64)[:, :, 63:64]
                S.copy(elast[:, hd, :], t1_.rearrange("p (c k) -> p c k", k=64)[:, :, 63])
                S.tt(kh.rearrange("p (c k) -> p c k", k=64), kt.rearrange("p (c k) -> p c k", k=64),
                     ebl.to_broadcast([128, 8, 64]), ALU.mult)

            def h_tail(hh):
                kh = H_TMP[hh % 2][4]
                for p in range(4):
                    S.tr(pT0[:, p * 128:(p + 1) * 128], kh[:, p * 128:(p + 1) * 128], ident, inc=(p == 3))
                S.copy(H_KTOK[hh], pT0[:, 0:512].rearrange("p (a b) -> p a b", b=128), eng="scalar")
                if do_out:
                    for p in range(2):
                        pc = slice(p * 128, (p + 1) * 128)
                        S.mm(pM[:, pc], H_KT[hh][:, pc], H_QT[hh][:, pc], inc=(p == 1))
                    S.tt(H_SCT[hh], pM[:, 0:256].rearrange("p (a b) -> p a b", b=128),
                         maskI.unsqueeze(1).to_broadcast([128, 2, 128]), ALU.mult)

            pend = h_proj(0)
            for hh in range(4):
                h_elem(hh, *pend)
                if hh < 3:
                    pend = h_proj(hh + 1)
                h_tail(hh)
            if gi == 0:
                PW["q1"] = load_w(wb_in, 0, 16, 512, 512, slot=1)
                PW["f1"] = load_w(wb_in, 0, 16, 1024 + 512, 512, slot=2)
            else:
                PW["gq0"] = load_w(wb_in, 0, 16, 4096, 512, slot=1)
                PW["gk0"] = load_w(wb_in, 0, 16, 5120, 512, slot=2)
            for c in range(8):
                p, r = c // 2, slice((c % 2) * 64, (c % 2) * 64 + 64)
                for hh in range(4):
                    hd = gi * 4 + hh
                    vv = hv[r, p, hd * 128:(hd + 1) * 128]
                    if do_out and c < 4:
                        oc = hb[hh][:, 256 + c * 64:256 + (c + 1) * 64]
                        S.mm(oc, Sbf[hd][:], H_QT[hh][:, c * 64:(c + 1) * 64], start=True, stop=False, inc=False)
                        S.mm(oc, vv, H_SCT[hh][r, p, (c % 2) * 64:(c % 2) * 64 + 64], start=False, stop=True)
                    S.mm(hb[hh][:, 128:256], H_KTOK[hh][r, p, :], vv)
                    S.stt(Sst[hd][:], Sst[hd][:], elast[:, hd, c:c + 1], hb[hh][:, 128:256], ALU.mult, ALU.add)
                    S.copy(Sbf[hd][:], Sst[hd][:], eng="scalar")
            if do_out:
                for hh in range(4):
                    head_norm_out(hb[hh][:, 256:512], Wg, hh * 128, 0, oT[:, gi * 4 + hh, :])
                if gi == 0:
                    PW["g1"] = load_w(wb_in, 0, 16, 3072 + 512, 512, slot=3)
                else:
                    PW["z0"] = load_w(wb_in, 0, 16, 7168, 512, slot=3)

        ckpt(f"h{tile}")
        fence(SCR_NAMES + MSC2_H + MSC2_G)
        for k in range(16):
            S.mm(pS[0:8, :], wab[:, k, 0:8], xnT[:, k, :], start=(k == 0), stop=(k == 15), inc=(k == 15))
        for k in range(16):
            S.mm(pM[0:8, :], wab[:, k, 8:16], xnT[:, k, :], start=(k == 0), stop=(k == 15), inc=(k == 15))
        S.act(gT[:], pS[0:8, :], AF.Exp, bias=gsc[:, 1:2])
        S.act(gT[:], gT[:], AF.Ln, bias=gsc[:, 3:4])
        S.ts(gT[:], gT[:], gsc[:, 2:3])
        S.scan(gcT[:], rst[0:8, :], gT[:], 0.0, ALU.mult, ALU.add)
        S.act(beT[:], pM[0:8, :], AF.Sigmoid)
        gl = gcT[:].rearrange("p (c k) -> p c k", k=64)[:, :, 63:64]
        S.tt(glT[:].rearrange("p (c k) -> p c k", k=64), gl.to_broadcast([8, 8, 64]),
             gcT[:].rearrange("p (c k) -> p c k", k=64), ALU.subtract)
        for p in range(4):
            pc = slice(p * 128, (p + 1) * 128)
            S.mm(pX[:, 0:8], gcT[:, pc], id8)
            S.mm(pX[:, 8:16], beT[:, pc], id8)
            S.mm(pX[:, 16:24], glT[:, pc], id8)
            S.copy(tok[:, p, :], pX[:, 0:24])
        S.ts(tok2[:, :, 0:8], tok[:, :, 0:8], -1.0)
        S.act(tok2[:, :, 24:32], tok[:, :, 0:8], AF.Exp)
        S.tt(tok2[:, :, 8:16], tok2[:, :, 24:32], tok[:, :, 8:16], ALU.mult)
        S.act(tok2[:, :, 16:24], tok[:, :, 16:24], AF.Exp)

        def bc4(ap128):
            return ap128.unsqueeze(1).to_broadcast([128, 4, 128])

        def tokbc(t, col):
            return t[:, :, col:col + 1].to_broadcast([128, 4, 128])

        def run_streams(*gens):
            gens = [g for g in gens if g is not None]
            while gens:
                for g in list(gens):
                    try:
                        next(g)
                    except StopIteration:
                        gens.remove(g)

        def g_s1(gi, hh, W3):
            hd = gi * 4 + hh
            st = hh % 2
            for qi, Wb in enumerate(W3):
                ci = qi * 8 + hd
                pp = next_p()
                acc = (G_QF[st], G_KF[st], G_VACC)[qi]
                if qi == 0:
                    Wd = T2
                    if do_out:
                        proj_fm(pp[:, 0:T2], Wb, hh * 128, lambda k: xnT[:, k, 0:T2])
                    proj_fm(pp[:, T2:T2 + 3], Wb, hh * 128, lambda k: xnT[:, k, T - 3:T])
                    tail = pp[:, T2:T2 + 3]
                else:
                    Wd = T
                    proj_fm(pp[:], Wb, hh * 128, lambda k: xnT[:, k, :])
                    tail = pp[:, T - 3:T]
                yield
                if qi > 0 or do_out:
                    S.ts(acc[:, 0:Wd], pp[:, 0:Wd], cw_t[:, 3, ci:ci + 1])
                    yield
                    for kq in range(3):
                        sh = 3 - kq
                        S.stt(acc[:, sh:Wd], pp[:, 0:Wd - sh], cw_t[:, kq, ci:ci + 1], acc[:, sh:Wd],
                              ALU.mult, ALU.add)
                        S.stt(acc[:, 0:sh], halo[:, ci, 4 - sh:4], cw_t[:, kq, ci:ci + 1], acc[:, 0:sh],
                              ALU.mult, ALU.add)
                        yield
                S.copy(halo[:, ci, 1:4], tail)
                if qi == 2:
                    S.act(G_VT[st], acc, AF.Silu)
                elif qi == 1 or do_out:
                    S.act(acc[:, 0:Wd], acc[:, 0:Wd], AF.Silu)
                yield

        def g_s23(gi, hh):
            hd = gi * 4 + hh
            si = 8 + hd
            st = hh % 2
            qf, kf, vt = G_QF[st], G_KF[st], G_VT[st]
            selh = cf[0:8, 8 + hd * 128: 8 + (hd + 1) * 128]
            S.mm(pX[:], selh, gcT[:])
            if do_out:
                S.act(G_QNT[:, 0:T2], qf[:, 0:T2], AF.Square)
            S.act(G_KBT, kf, AF.Square)
            yield
            S.copy(G_GCB, pX[:], eng="scalar")
            S.act(G_EGC, pX[:], AF.Exp)
            S.act(elast[:, si, :], pX[:].rearrange("p (c k) -> p c k", k=64)[:, :, 63], AF.Exp)
            if do_out:
                S.mm(pO[:, 0:T2], ones_bf, G_QNT[:, 0:T2])
            S.mm(pS[:], ones_bf, G_KBT)
            yield
            if do_out:
                S.act(G_LNRQ[:, 0:T2], pO[:, 0:T2], AF.Ln, bias=eps_ap)
            S.act(G_LNRK, pS[:], AF.Ln, bias=eps_ap)
            yield
            if do_out:
                S.act(G_LNRQ[:, 0:T2], G_LNRQ[:, 0:T2], AF.Exp, scale=-0.5)
            S.act(G_LNRK, G_LNRK, AF.Exp, scale=-0.5)
            S.mm(pX[:], selh, beT[:])
            yield
            if do_out:
                S.stt(G_QNT[:, 0:T2], qf[:, 0:T2], float(128 ** -0.5), G_LNRQ[:, 0:T2], ALU.mult, ALU.mult)
            S.tt(G_KNT, kf, G_LNRK, ALU.mult)
            yield
            S.tt(G_KBT, G_KNT, pX[:], ALU.mult)
            if do_out:
                S.tt(G_QG[hh], G_QNT[:, 0:T2], G_EGC[:, 0:T2], ALU.mult)
            for p in range(4):
                S.tr(pT0[:, p * 128:(p + 1) * 128], G_KNT[:, p * 128:(p + 1) * 128], ident, inc=(p == 3))
            for p in range(4):
                S.tr(pT1[:, p * 128:(p + 1) * 128], vt[:, p * 128:(p + 1) * 128], ident, inc=(p == 3))
            yield
            pk4 = pT0[:, 0:512].rearrange("p (a b) -> p a b", b=128)
            pv4 = pT1[:, 0:512].rearrange("p (a b) -> p a b", b=128)
            for p in range(4):
                pc = slice(p * 128, (p + 1) * 128)
                S.mm(pM[:, pc], G_KNT[:, pc], G_KBT[:, pc], inc=(p == 3))
            for p in range(4):
                pc = slice(p * 128, (p + 1) * 128)
                S.mm(pX[:, pc], G_KNT[:, pc], G_KNT[:, pc], inc=(p == 3))
            if do_out:
                for p in range(2):
                    pc = slice(p * 128, (p + 1) * 128)
                    S.mm(pS[:, pc], G_KNT[:, pc], G_QNT[:, pc], inc=(p == 1))
            S.tt(G_KB4, pk4, tokbc(tok2, 8 + hd), ALU.mult)
            yield
            gcb4 = G_GCB.rearrange("p (a b) -> p a b", b=128)
            S.tt(G_DM, gcb4, bc4(negI), ALU.add)
            S.tt(G_DT, bc4(negIT), gcb4, ALU.subtract)
            yield
            S.tt(G_DM, G_DM, tokbc(tok2, hd), ALU.add)
            S.tt(G_DT, G_DT, tokbc(tok, hd), ALU.add)
            yield
            S.act(G_DM, G_DM, AF.Exp)
            S.act(G_DT, G_DT, AF.Exp)
            S.tt(G_KH4[hh], pk4, tokbc(tok2, 16 + hd), ALU.mult)
            S.tt(G_BV4[hh], pv4, tokbc(tok, 8 + hd), ALU.mult)
            yield
            if do_out:
                S.tt(G_AT4[hh], pS[:, 0:256].rearrange("p (a b) -> p a b", b=128), G_DM[:, 0:2, :], ALU.mult)
            S.tt(G_DT, G_DT, bc4(mST), ALU.mult)
            yield
            S.tt(G_DM, G_DM, bc4(mS), ALU.mult)
            S.tt(G_DT, G_DT, tokbc(tok, 8 + hd), ALU.mult)
            yield
            S.tt(G_X4, pM[:].rearrange("p (a b) -> p a b", b=128), G_DM, ALU.mult)
            S.tt(G_XT4, pX[:].rearrange("p (a b) -> p a b", b=128), G_DT, ALU.mult)
            yield
            P4 = G_P4[hh]
            S.tt(P4, G_X4, bc4(ident), ALU.add)
            xk, xtk = G_X4, G_XT4
            alt = [(G_XA4, G_XB4), (G_X4, G_XT4)]
            for lvl in range(1, 6):
                nx, nxt = alt[(lvl - 1) % 2]
                for p in range(4):
                    S.mm(pX[:, p * 128:(p + 1) * 128], xk[:, p, :], xtk[:, p, :], inc=(p == 3))
                if lvl < 5:
                    for p in range(4):
                        S.mm(pM[:, p * 128:(p + 1) * 128], xtk[:, p, :], xk[:, p, :], inc=(p == 3))
                yield
                S.copy(nxt, pX[:].rearrange("p (a b) -> p a b", b=128))
                if lvl < 5:
                    S.copy(nx, pM[:].rearrange("p (a b) -> p a b", b=128), eng="scalar")
                yield
                for p in range(4):
                    S.mm(pS[:, p * 128:(p + 1) * 128], nxt[:, p, :], P4[:, p, :], inc=(p == 3))
                yield
                S.tt(P4, P4, pS[:].rearrange("p (a b) -> p a b", b=128), ALU.add)
                yield
                xk, xtk = nx, nxt
            for p in range(4):
                S.mm(pM[:, p * 128:(p + 1) * 128], G_KB4[:, p, :], P4[:, p, :], inc=(p == 3))
            yield
            S.ts(G_NW4[hh], pM[:].rearrange("p (a b) -> p a b", b=128), -1.0)
            yield

        for gi in range(2):
            W3 = (PW[f"gq{gi}"], PW[f"gk{gi}"], PW[f"gv{gi}"])
            run_streams(g_s1(gi, 0, W3))
            for hh in range(4):
                run_streams(g_s23(gi, hh), g_s1(gi, hh + 1, W3) if hh < 3 else None)
                if hh == 2 and gi == 0:
                    PW["gq1"] = load_w(wb_in, 0, 16, 4096 + 512, 512, slot=1)
                    PW["gk1"] = load_w(wb_in, 0, 16, 5120 + 512, 512, slot=2)
                    PW["gv1"] = load_w(wb_in, 0, 16, 6144 + 512, 512, slot=0)
            for c in range(8):
                p, r = c // 2, slice((c % 2) * 64, (c % 2) * 64 + 64)
                for hh in range(4):
                    si = 8 + gi * 4 + hh
                    S.mm(hb[hh][:, 0:128], G_P4[hh][r, p, :], G_BV4[hh][r, p, :], start=True, stop=False, inc=False)
                    S.mm(hb[hh][:, 0:128], G_NW4[hh][:, p, :], Sbf[si][:], start=False, stop=True)
                    vn = G_VN[hh]
                    S.copy(vn[r, :], hb[hh][r, 0:128])
                    if do_out and c < 4:
                        oc = hb[hh][:, 256 + c * 64:256 + (c + 1) * 64]
                        S.mm(oc, Sbf[si][:], G_QG[hh][:, c * 64:(c + 1) * 64], start=True, stop=False, inc=False)
                        S.mm(oc, vn[r, :], G_AT4[hh][r, p, (c % 2) * 64:(c % 2) * 64 + 64], start=False, stop=True)
                    S.mm(hb[hh][:, 128:256], G_KH4[hh][r, p, :], vn[r, :])
                    S.stt(Sst[si][:], Sst[si][:], elast[:, si, c:c + 1], hb[hh][:, 128:256], ALU.mult, ALU.add)
                    S.copy(Sbf[si][:], Sst[si][:], eng="scalar")
            if do_out:
                Wz = PW[f"z{gi}"]
                for hh in range(4):
                    head_norm_out(hb[hh][:, 256:512], Wz, hh * 128, 1, oT[:, 8 + gi * 4 + hh, :])
                if gi == 0:
                    PW["z1"] = load_w(wb_in, 0, 16, 7168 + 512, 512, slot=3)

        ckpt(f"g{tile}")
        if debug and tile >= 1:
            S.dma("sync", dbg_o[tile], oT[:].rearrange("p h t -> p (h t)"))
        if tile == 0:
            continue

        for s in range(2):
            S.dma("sync", h1v[s], xs[tok0 + s * 128: tok0 + (s + 1) * 128, :])
        ring_i[0] = 0
        for mb2 in range(8):
            hs0 = 4 * (mb2 % 2)
            Wga = load_half(wb_in, 16, 8208 + mb2 * 256, hs0 + 0)
            Wgb = load_half(wb_in, 16, 10256 + mb2 * 256, hs0 + 1)
            Wba = load_half(wb_ba, 8, mb2 * 256, hs0 + 2)
            Wbb = load_half(wb_bb, 8, mb2 * 256, hs0 + 3)
            for j in range(2):
                m = mb2 * 2 + j
                pga = next_p()
                proj_fm(pga[:, 0:T2], Wga, j * 128, lambda k: xnT[:, k, 0:T2])
                sga = tp[0][:]
                S.act(sga, pga[:, 0:T2], AF.Sigmoid)
                pgb = next_p()
                proj_fm(pgb[:, 0:T2], Wgb, j * 128, lambda k: xnT[:, k, 0:T2])
                sgb = tp[1][:]
                S.act(sgb, pgb[:, 0:T2], AF.Sigmoid)
                proj_fm(pO[:, 0:T2], Wba, j * 128, lambda k: oT[:, k, :], nk=8)
                proj_fm(pS[:, 0:T2], Wbb, j * 128, lambda k: oT[:, 8 + k, :], nk=8)
                t1, t2 = tp[2][:], tp[3][:]
                S.tt(t1, pO[:, 0:T2], sga, ALU.mult)
                S.tt(t2, pS[:, 0:T2], sgb, ALU.mult)
                S.tt(mergedT[:, m, :], t1, t2, ALU.add, eng="gpsimd")
        ckpt(f"m{tile}")
        for nb in range(4):
            Wo = load_w(wb_o, 0, 16, nb * 512, 512)
            for s in range(2):
                po = next_p()
                for m in range(16):
                    S.mm(po[:], mergedT[:, m, s * 128:(s + 1) * 128], Wo[:, m, :], start=(m == 0),
                         stop=(m == 15), inc=(m == 15))
                hs = h1v[s][:, nb * 512:(nb + 1) * 512]
                S.tt(hs, po[:], hs, ALU.add)
        ckpt(f"o{tile}")
        for s in range(2):
            norm_transpose(h1v[s], s, 1, xn2T, 8)
        fence()
        for fb in range(11):
            Wg_ = load_w(wb_fi, 0, 16, fb * 512, 512)
            Wu_ = load_w(wb_fi, 0, 16, DFF + fb * 512, 512)
            for j in range(4):
                f = fb * 4 + j
                pg = next_p()
                proj_fm(pg[:, 0:T2], Wg_, j * 128, lambda k: xn2T[:, k, :])
                sgt = tp[j % 2][:]
                S.act(sgt, pg[:, 0:T2], AF.Silu)
                pu = next_p()
                proj_fm(pu[:, 0:T2], Wu_, j * 128, lambda k: xn2T[:, k, :])
                S.tt(actT[:, f, :], pu[:, 0:T2], sgt, ALU.mult)
        ckpt(f"i{tile}")
        for nb in range(4):
            pys = [pO, pS]
            for kb4 in range(4):
                Wfo = load_w(wb_fo, kb4 * 11, 11, nb * 512, 512)
                for s in range(2):
                    for kk_ in range(11):
                        f = kb4 * 11 + kk_
                        S.mm(pys[s][:], actT[:, f, s * 128:(s + 1) * 128], Wfo[:, kk_, :],
                             start=(f == 0), stop=(f == 43), inc=(kk_ == 10))
            for s in range(2):
                hs = h1v[s][:, nb * 512:(nb + 1) * 512]
                S.tt(hs, pys[s][:], hs, ALU.add)
        fence()
        ckpt(f"f{tile}")
        for s in range(2):
            ssc = small[:, 32 + s:33 + s]
            S.act(xsb[:], h1v[s], AF.Square, accum_out=ssc)
            rs = small[:, 40 + s:41 + s]
            rstd_from_ss(rs, ssc, 1.0 / D)
            S.stt(h1v[s], h1v[s], rs, fnw_bc[:], ALU.mult, ALU.mult)
            r0 = (tile - 1) * T2 + s * 128
            last = S.dma("sync", out[r0:r0 + 128, :], h1v[s])

    try:
        main_body()
    except _Stop:
        pass
    for q in ("gpsimd", "sync"):
        n = S.dma_i[q]
        S.final_wait(q, [(("d", q, sl), 16 * ((n - 1 - sl) // NDMA + 1)) for sl in range(min(NDMA, n))])
    S.emit()
    return nc, S


def prep_inputs(inputs, NT=NT_FULL, seq=None):
    x = np.asarray(inputs["x"], np.float32)
    B = x.shape[0]
    if seq is None:
        seq = x.shape[1]
    meta = np.asarray(inputs["meta_tokens"], np.float32)
    w_in = np.ascontiguousarray(np.asarray(inputs["w_in"], np.float32)[0])
    cbf, cf32 = make_consts()
    shared = {
        "w_in": w_in,
        "w_ba": np.ascontiguousarray(inputs["w_branch_a"][0], np.float32),
        "w_bb": np.ascontiguousarray(inputs["w_branch_b"][0], np.float32),
        "w_o": np.ascontiguousarray(inputs["w_out"][0], np.float32),
        "w_fi": np.ascontiguousarray(inputs["w_ffn_in"][0], np.float32),
        "w_fo": np.ascontiguousarray(inputs["w_ffn_out"][0], np.float32),
        "w_ab": np.ascontiguousarray(w_in[:, 8192:8208]),
        "lbl": np.ascontiguousarray(np.asarray(inputs["lb_logits"], np.float32).reshape(2, 8, 128).transpose(2, 0, 1)),
        "nw3": np.ascontiguousarray(np.stack([np.asarray(inputs["mix_norm_w"], np.float32)[0].reshape(16, 128).T,
                                              np.asarray(inputs["ffn_norm_w"], np.float32)[0].reshape(16, 128).T], axis=1)),
        "fnw": np.ascontiguousarray(np.asarray(inputs["final_norm_w"], np.float32).reshape(1, D)),
        "hnw": np.ascontiguousarray(np.stack([np.asarray(inputs["hg_norm_w"], np.float32)[0],
                                              np.asarray(inputs["gd_norm_w"], np.float32)[0]], axis=1)),
        "cw": np.ascontiguousarray(np.asarray(inputs["gd_conv_w"], np.float32)[0].reshape(4, 24, 128).transpose(2, 0, 1)),
        "alog": np.ascontiguousarray(np.asarray(inputs["gd_a_log"], np.float32)[0].reshape(8, 1)),
        "dtb": np.ascontiguousarray(np.asarray(inputs["gd_dt_bias"], np.float32)[0].reshape(8, 1)),
        "cbf": cbf, "cf32": cf32,
    }
    maps = []
    NTOK = NT * T
    for c in range(2 * B):
        b, par = c // 2, c % 2
        Z = 496 if par == 0 else 240
        st = np.zeros((NTOK, D), np.float32)
        st[Z:Z + 16] = meta
        n = min(seq, NTOK - Z - 16)
        st[Z + 16:Z + 16 + n] = x[b, :n]
        m = dict(shared)
        m["xs"] = st
        maps.append(m)
    return maps


def assemble(results, B, seq, NT=NT_FULL):
    outp = np.zeros((B, seq, D), np.float32)
    for c in range(2 * B):
        b, par = c // 2, c % 2
        o = np.asarray(results[c]["out"], np.float32)
        for tile in range(1, NT):
            r0 = (tile - 1) * T + par * T2
            if r0 >= seq:
                continue
            outp[b, r0:r0 + T2] = o[(tile - 1) * T2: tile * T2]
    return outp


_CACHE = {}


def kernel(**inputs):
    x = np.asarray(inputs["x"])
    B, seq = x.shape[0], x.shape[1]
    if "nc" not in _CACHE:
        _CACHE["nc"] = build(NT_FULL)[0]
    nc = _CACHE["nc"]
    maps = prep_inputs(inputs, NT_FULL)
    res = run_bass_kernel_spmd(nc, maps, core_ids=list(range(2 * B)))
    return assemble(res.results, B, seq, NT_FULL)
```
